# Optimizing a Trainium2 kernel written in Bass

```python
import jax, jax.numpy as jnp
from jax import lax
import numpy as np

D_MODEL = 4096
BATCH = 2
SEQ = 4096
DEPTH = 1

HEAD_DIM = 128
DIL_CONFIGS = ((128, 1), (512, 4), (2048, 16))
N_DIL_GROUPS = 3
DIL_HEADS = 8
DIL_WIDTH = N_DIL_GROUPS * DIL_HEADS * HEAD_DIM
DIL_OUT_WIDTH = DIL_HEADS * HEAD_DIM
DIFF_HEADS = 8
DIFF_QK_WIDTH = DIFF_HEADS * 2 * HEAD_DIM
DIFF_V_DIM = 2 * HEAD_DIM
DIFF_V_WIDTH = DIFF_HEADS * DIFF_V_DIM
MEM_LEN = 256
MEM_HEADS = 4
MEM_WIDTH = MEM_HEADS * HEAD_DIM
D_FF = 4 * D_MODEL
ROPE_THETA = 10000.0
Q_BLOCK = 128
NORM_EPS = 1e-6
MASK_VALUE = -1e30
IN_SPLITS = (DIL_WIDTH, DIL_WIDTH, DIL_WIDTH,
             DIFF_QK_WIDTH, DIFF_QK_WIDTH, DIFF_V_WIDTH,
             D_MODEL, D_MODEL)
D_IN = DIL_WIDTH * 3 + DIFF_QK_WIDTH * 2 + DIFF_V_WIDTH + 2 * D_MODEL

kernel_name = "hybrid_dilated_diffattn_gated_encoder"


def _split_points():
    pts, acc = [], 0
    for s in IN_SPLITS[:-1]:
        acc += s
        pts.append(acc)
    return pts


def rms_norm(x, g):
    xf = x.astype(jnp.float32)
    y = xf * lax.rsqrt(jnp.mean(xf * xf, axis=-1, keepdims=True) + NORM_EPS)
    return (y * g.astype(jnp.float32)).astype(x.dtype)


def rope(x, positions):
    d = x.shape[-1]
    inv_freq = ROPE_THETA ** (-jnp.arange(0, d, 2, dtype=jnp.float32) / d)
    ang = positions.astype(jnp.float32)[..., None] * inv_freq
    ang = ang.reshape(ang.shape[:2] + (1,) * (x.ndim - 3) + (d // 2,))
    cos, sin = jnp.cos(ang), jnp.sin(ang)
    xf = x.astype(jnp.float32)
    x1, x2 = xf[..., : d // 2], xf[..., d // 2:]
    out = jnp.concatenate([x1 * cos - x2 * sin, x2 * cos + x1 * sin], axis=-1)
    return out.astype(x.dtype)


def dilated_attention(q, k, v):
    b, s, g_n, h_n, d = q.shape
    nb = s // Q_BLOCK
    scale = HEAD_DIM ** -0.5
    q_blocks = q.reshape(b, nb, Q_BLOCK, g_n, h_n, d).transpose(1, 0, 2, 3, 4, 5)

    def one_block(args):
        q_blk, blk = args
        qpos = blk * Q_BLOCK + jnp.arange(Q_BLOCK)
        outs, lses = [], []
        for g, (window, dil) in enumerate(DIL_CONFIGS):
            n_side = window // (2 * dil)
            offsets = jnp.arange(-n_side, n_side + 1) * dil
            idx = qpos[:, None] + offsets[None, :]
            valid = (idx >= 0) & (idx < s)
            idx = jnp.clip(idx, 0, s - 1)
            k_sel = k[:, :, g][:, idx]
            v_sel = v[:, :, g][:, idx]
            sc = jnp.einsum('bqhd,bqjhd->bhqj', q_blk[:, :, g], k_sel,
                            preferred_element_type=jnp.float32) * scale
            sc = jnp.where(valid[None, None], sc, MASK_VALUE)
            m = jnp.max(sc, axis=-1, keepdims=True)
            p = jnp.exp(sc - m)
            l = jnp.sum(p, axis=-1, keepdims=True)
            o = jnp.einsum('bhqj,bqjhd->bqhd', p, v_sel.astype(jnp.float32))
            o = o / l.transpose(0, 2, 1, 3)
            outs.append(o)
            lses.append((m + jnp.log(l))[..., 0].transpose(0, 2, 1))
        o_all = jnp.stack(outs, axis=0)
        alpha = jax.nn.softmax(jnp.stack(lses, axis=0), axis=0)[..., None]
        return jnp.sum(alpha * o_all, axis=0).astype(q.dtype)

    out = lax.map(one_block, (q_blocks, jnp.arange(nb)))
    return out.transpose(1, 0, 2, 3, 4).reshape(b, s, h_n * d)


def differential_attention(q, k, v, lam, lam_init, subln):
    b, s, h_n, _, d = q.shape
    nb = s // Q_BLOCK
    scale = HEAD_DIM ** -0.5
    q_blocks = q.reshape(b, nb, Q_BLOCK, h_n, 2, d).transpose(1, 0, 2, 3, 4, 5)
    v32 = v.astype(jnp.float32)

    def one_block(q_blk):
        sc = jnp.einsum('bqhcd,bkhcd->bchqk', q_blk, k,
                        preferred_element_type=jnp.float32) * scale
        a = jax.nn.softmax(sc, axis=-1)
        attn = a[:, 0] - lam * a[:, 1]
        o = jnp.einsum('bhqk,bkhe->bqhe', attn, v32)
        o = rms_norm(o, subln) * (1.0 - lam_init)
        return o.astype(q.dtype)

    out = lax.map(one_block, q_blocks)
    return out.transpose(1, 0, 2, 3, 4).reshape(b, s, h_n * DIFF_V_DIM)


def memory_cross_attention(h, m, w_q, w_kv, w_o):
    b, s, _ = h.shape
    q = (h @ w_q).reshape(b, s, MEM_HEADS, HEAD_DIM)
    kv = m @ w_kv
    k, v = jnp.split(kv, 2, axis=-1)
    k = k.reshape(b, m.shape[1], MEM_HEADS, HEAD_DIM)
    v = v.reshape(b, m.shape[1], MEM_HEADS, HEAD_DIM)
    sc = jnp.einsum('bshd,bmhd->bhsm', q, k,
                    preferred_element_type=jnp.float32) * HEAD_DIM ** -0.5
    a = jax.nn.softmax(sc, axis=-1)
    o = jnp.einsum('bhsm,bmhd->bshd', a, v.astype(jnp.float32)).astype(h.dtype)
    return o.reshape(b, s, MEM_WIDTH) @ w_o


def setup_inputs(seed: int = 0) -> dict:
    key = jax.random.key(seed)
    ks = jax.random.split(key, 24)
    f32 = jnp.float32

    def dense(k, fan_in, fan_out):
        return jax.random.normal(k, (DEPTH, fan_in, fan_out), f32) * fan_in ** -0.5

    def gain(k, n):
        return 1.0 + 0.05 * jax.random.normal(k, (DEPTH, n), f32)

    return {
        "x": jax.random.normal(ks[0], (BATCH, SEQ, D_MODEL), f32),
        "mem": jax.random.normal(ks[1], (BATCH, MEM_LEN, D_MODEL), f32),
        "positions": (jnp.arange(SEQ, dtype=jnp.int32)[None, :]
                      + jax.random.randint(ks[2], (BATCH, 1), 0, 1024, dtype=jnp.int32)),
        "norm_mix_pre": gain(ks[3], D_MODEL),
        "w_in": dense(ks[4], D_MODEL, D_IN),
        "w_a": dense(ks[5], DIL_OUT_WIDTH, D_MODEL),
        "w_b": dense(ks[6], DIFF_V_WIDTH, D_MODEL),
        "w_mix_out": dense(ks[7], D_MODEL, D_MODEL),
        "norm_mix_post": gain(ks[8], D_MODEL),
        "lambda_q1": 0.1 * jax.random.normal(ks[9], (DEPTH, HEAD_DIM), f32),
        "lambda_k1": 0.1 * jax.random.normal(ks[10], (DEPTH, HEAD_DIM), f32),
        "lambda_q2": 0.1 * jax.random.normal(ks[11], (DEPTH, HEAD_DIM), f32),
        "lambda_k2": 0.1 * jax.random.normal(ks[12], (DEPTH, HEAD_DIM), f32),
        "diff_subln": gain(ks[13], DIFF_V_DIM),
        "norm_mem_pre": gain(ks[14], D_MODEL),
        "norm_mem_kv": gain(ks[15], D_MODEL),
        "w_mem_q": dense(ks[16], D_MODEL, MEM_WIDTH),
        "w_mem_kv": dense(ks[17], D_MODEL, 2 * MEM_WIDTH),
        "w_mem_o": dense(ks[18], MEM_WIDTH, D_MODEL),
        "norm_mem_post": gain(ks[19], D_MODEL),
        "norm_mlp_pre": gain(ks[20], D_MODEL),
        "w_mlp_up": dense(ks[21], D_MODEL, D_FF),
        "w_mlp_down": dense(ks[22], D_FF, D_MODEL),
        "norm_mlp_post": gain(ks[23], D_MODEL),
    }


def reference(x, mem, positions, norm_mix_pre, w_in, w_a, w_b, w_mix_out, norm_mix_post,
              lambda_q1, lambda_k1, lambda_q2, lambda_k2, diff_subln,
              norm_mem_pre, norm_mem_kv, w_mem_q, w_mem_kv, w_mem_o, norm_mem_post,
              norm_mlp_pre, w_mlp_up, w_mlp_down, norm_mlp_post):
    b, s, _ = x.shape
    for layer in range(DEPTH):
        h = rms_norm(x, norm_mix_pre[layer])
        proj = h @ w_in[layer]
        qa, ka, va, qb, kb, vb, ga, gb = jnp.split(proj, _split_points(), axis=-1)

        qa = rope(qa.reshape(b, s, N_DIL_GROUPS, DIL_HEADS, HEAD_DIM), positions)
        ka = rope(ka.reshape(b, s, N_DIL_GROUPS, DIL_HEADS, HEAD_DIM), positions)
        va = va.reshape(b, s, N_DIL_GROUPS, DIL_HEADS, HEAD_DIM)
        out_a = dilated_attention(qa, ka, va)

        qb = rope(qb.reshape(b, s, DIFF_HEADS, 2, HEAD_DIM), positions)
        kb = rope(kb.reshape(b, s, DIFF_HEADS, 2, HEAD_DIM), positions)
        vb = vb.reshape(b, s, DIFF_HEADS, DIFF_V_DIM)
        lam_init = 0.8 - 0.6 * float(np.exp(-0.3 * layer))
        lam = (jnp.exp(jnp.sum(lambda_q1[layer].astype(jnp.float32) * lambda_k1[layer].astype(jnp.float32)))
               - jnp.exp(jnp.sum(lambda_q2[layer].astype(jnp.float32) * lambda_k2[layer].astype(jnp.float32)))
               + lam_init)
        out_b = differential_attention(qb, kb, vb, lam, lam_init, diff_subln[layer])

        merged = (jax.nn.sigmoid(ga) * (out_a @ w_a[layer])
                  + jax.nn.sigmoid(gb) * (out_b @ w_b[layer]))
        x = x + rms_norm(merged @ w_mix_out[layer], norm_mix_post[layer])

        h = rms_norm(x, norm_mem_pre[layer])
        m = rms_norm(mem, norm_mem_kv[layer])
        y = memory_cross_attention(h, m, w_mem_q[layer], w_mem_kv[layer], w_mem_o[layer])
        x = x + rms_norm(y, norm_mem_post[layer])

        h = rms_norm(x, norm_mlp_pre[layer])
        u = jnp.square(jax.nn.relu(h @ w_mlp_up[layer]))
        x = x + rms_norm(u @ w_mlp_down[layer], norm_mlp_post[layer])
    return x
```

```python
import math
from contextlib import ExitStack

import numpy as np
import concourse.bass as bass
import concourse.mybir as mybir
from concourse.bass_utils import run_bass_kernel_spmd

F32 = mybir.dt.float32
BF16 = mybir.dt.bfloat16
I32 = mybir.dt.int32
ALU = mybir.AluOpType
AF = mybir.ActivationFunctionType
AX = mybir.AxisListType

EPS = 1e-6
PI = math.pi
PI_SAFE = 3.141592
LAM_INIT = 0.8 - 0.6 * 1.0
DIL_CFG = ((128, 1), (512, 4), (2048, 16))
NCORES = 8
import os
EVAC_MODE = int(os.environ.get('EVAC_MODE', '1'))
POOL_ENG = os.environ.get('POOL_ENG', 'pool')
MLP_EXTRA = int(os.environ.get('MLP_EXTRA', '2'))
EVAC_FUNC = AF.Identity if os.environ.get('EVAC_ID') else AF.Copy
OWN = 1024


class Cfg:
    def __init__(s, D=4096, HA=8, HB=8, HM=4, FF=16384, S=4096, MEM=256):
        s.D, s.HA, s.HB, s.HM, s.FF, s.S, s.MEM = D, HA, HB, HM, FF, S, MEM
        s.G = 3
        s.KC = D // 128
        s.QAW = 3 * HA * 128
        s.QBW = HB * 256
        s.o_qa = 0
        s.o_ka = s.QAW
        s.o_va = 2 * s.QAW
        s.o_qb = 3 * s.QAW
        s.o_kb = s.o_qb + s.QBW
        s.o_vb = s.o_kb + s.QBW
        s.o_ga = s.o_vb + s.QBW
        s.o_gb = s.o_ga + D
        s.DIN = s.o_gb + D
        s.AW = HA * 128
        s.BW = HB * 256
        s.MW = HM * 128
        s.NQH = 3 * HA + 2 * HB
        s.SLAB = min(2048, FF)


def dil_chunk_plan():
    masks = []
    plan = []
    for g, (win, dil) in enumerate(DIL_CFG):
        half = win // 2
        lo = -((half + 127) // 128) * 128
        hi = 512 + ((half + 127) // 128) * 128
        lst = []
        interior = None
        for delta in range(lo, hi, 128):
            is_int = (delta + 127 <= half) and (delta - 511 >= -half)
            if is_int and interior is not None:
                lst.append((delta, interior))
                continue
            masks.append((g, delta))
            if is_int:
                interior = len(masks) - 1
            lst.append((delta, len(masks) - 1))
        plan.append(lst)
    return plan, masks


def build_masks():
    plan, masks = dil_chunk_plan()
    out = np.zeros((len(masks), 128, 512), np.float32)
    p = np.arange(128)[:, None]
    f = np.arange(512)[None, :]
    for i, (g, delta) in enumerate(masks):
        win, dil = DIL_CFG[g]
        d = delta + p - f
        out[i] = ((np.abs(d) <= win // 2) & (d % dil == 0)).astype(np.float32)
    return out


class T:
    __slots__ = ("w", "r", "multi", "sem", "cnt")

    def __init__(s, multi=False):
        s.w = {}
        s.r = {}
        s.multi = multi
        s.sem = None
        s.cnt = 0


ENGS = ("pe", "act", "dve", "pool", "sp")


class Sched:
    def __init__(s, nc, es):
        s.nc = nc
        s.es = es
        s.prog = {e: [] for e in ENGS}
        s.esem = {e: es.enter_context(nc.semaphore("es_" + e)) for e in ENGS}
        s.cnt = {e: 0 for e in ENGS}
        s.seen = {e: {} for e in ENGS}
        s.semname = {}
        s.nsem = 0
        s.dma_sems = []
        for e in ENGS:
            s.semname[id(s.esem[e])] = s.esem[e]

    def _need(s, eng, rd, wr):
        deps = {}

        def mg(d):
            for k, v in d.items():
                if deps.get(k, 0) < v:
                    deps[k] = v

        for t in rd:
            mg(t.w)
        for t in wr:
            mg(t.r)
            if not t.multi:
                mg(t.w)
        own = id(s.esem[eng])
        seen = s.seen[eng]
        for k, v in deps.items():
            if k == own and eng == "pe":
                continue
            if seen.get(k, 0) < v:
                seen[k] = v
                s.prog[eng].append(("w", s.semname[k], v))

    def _record(s, ev, rd, wr):
        k, v = ev
        for t in rd:
            if t.r.get(k, 0) < v:
                t.r[k] = v
        for t in wr:
            if t.w.get(k, 0) < v:
                t.w[k] = v

    def op(s, eng, fn, rd=(), wr=(), inc=True):
        s._need(eng, rd, wr)
        if inc:
            s.cnt[eng] += 1
            ev = (id(s.esem[eng]), s.cnt[eng])
        else:
            ev = (id(s.esem[eng]), s.cnt[eng] + 1)
        s.prog[eng].append(("o", fn, s.esem[eng] if inc else None))
        s._record(ev, rd, wr)

    def dma(s, q, out, in_, rd=(), wr=(), semT=None, slow=False):
        s._need(q, rd, wr)
        if semT.sem is None:
            semT.sem = {}
            semT.cnt = {}
        if q not in semT.sem:
            s.nsem += 1
            sem = s.es.enter_context(s.nc.semaphore("ds%d" % s.nsem))
            semT.sem[q] = sem
            semT.cnt[q] = 0
            s.semname[id(sem)] = sem
            s.dma_sems.append((semT, q))
        semT.cnt[q] += 16
        s.prog[q].append(("d", out, in_, semT.sem[q], slow))
        s._record((id(semT.sem[q]), semT.cnt[q]), rd, wr)

    def barrier(s, new_epoch=False):
        s._barrier()
        if new_epoch:
            for e in ENGS:
                s.esem[e] = s.es.enter_context(s.nc.semaphore("es_%s_%d" % (e, s.nsem)))
                s.nsem += 1
                s.semname[id(s.esem[e])] = s.esem[e]
                s.cnt[e] = 0

    def _barrier(s):
        for e in ENGS:
            seen = s.seen[e]
            for e2 in ENGS:
                if e2 == e:
                    continue
                k = id(s.esem[e2])
                v = s.cnt[e2]
                if v > 0 and seen.get(k, 0) < v:
                    seen[k] = v
                    s.prog[e].append(("w", s.esem[e2], v))
            for (t, q) in s.dma_sems:
                k = id(t.sem[q])
                v = t.cnt[q]
                if v > 0 and seen.get(k, 0) < v:
                    seen[k] = v
                    s.prog[e].append(("w", t.sem[q], v))

    def emit(s, block):
        def replay(eng_name):
            lst = s.prog[eng_name]

            def f(e):
                for it in lst:
                    if it[0] == "w":
                        e.wait_ge(it[1], it[2])
                    elif it[0] == "o":
                        ins = it[1](e)
                        if it[2] is not None:
                            ins.then_inc(it[2], 1)
                    elif it[4]:
                        e.dma_start(out=it[1], in_=it[2], allow_slow_non_contiguous=True).then_inc(it[3], 16)
                    else:
                        e.dma_start(out=it[1], in_=it[2]).then_inc(it[3], 16)

            return f

        block.tensor(replay("pe"))
        block.scalar(replay("act"))
        block.vector(replay("dve"))
        block.gpsimd(replay("pool"))
        block.sync(replay("sp"))


class Ring:
    def __init__(s, items):
        s.items = items
        s.i = 0

    def next(s):
        it = s.items[s.i % len(s.items)]
        s.i += 1
        return it


class KB:
    pass


def build(cfg, stop=None):
    c = cfg
    D, KC, S = c.D, c.KC, c.S
    nc = bass.Bass("TRN2", target_bir_lowering=False)
    K = KB()
    K.nc, K.c = nc, c
    plan, mask_list = dil_chunk_plan()
    NM = len(mask_list)

    def din(name, shape, dt=F32):
        return nc.dram_tensor(name, list(shape), dt, kind="ExternalInput")

    def dscr(name, shape, dt):
        return nc.dram_tensor(name, list(shape), dt, kind="Internal")

    xr = din("xr", [S, D])
    pos = din("pos", [1, S], I32)
    valid = din("valid", [128, S // 128])
    memx = din("mem", [c.MEM, D])
    w_in = din("w_in", [D, c.DIN])
    w_a = din("w_a", [c.AW, D])
    w_b = din("w_b", [c.BW, D])
    w_mix = din("w_mix", [D, D])
    w_mq = din("w_mq", [D, c.MW])
    w_mkv = din("w_mkv", [D, 2 * c.MW])
    w_mo = din("w_mo", [c.MW, D])
    w_up = din("w_up", [D, c.FF])
    w_dn = din("w_dn", [c.FF, D])
    gnames = ["g_mix_pre", "g_mix_post", "g_mem_pre", "g_mem_kv", "g_mem_post", "g_mlp_pre", "g_mlp_post"]
    gv = {n: din(n, [1, D]) for n in gnames}
    gpc_in = {n: din("pc_" + n, [128, KC]) for n in ("g_mix_pre", "g_mem_pre", "g_mem_kv", "g_mlp_pre")}
    subln = din("subln", [1, 256])
    lam_in = {n: din(n, [1, 128]) for n in ("lq1", "lk1", "lq2", "lk2")}
    ident_in = din("ident", [128, 128])
    invf_in = din("invf", [128, 1])
    sgn_in = din("sgn", [128, 1])
    masks_in = din("masks", [NM, 128, 512])
    out = nc.dram_tensor("out", [OWN, D], F32, kind="ExternalOutput")

    QT = dscr("QT", [c.NQH, 128, OWN], BF16)
    KT = dscr("KT", [c.NQH, 128, S], BF16)
    VA = dscr("VA", [3 * c.HA, S, 128], BF16)
    VB = dscr("VB", [c.HB, S, 256], BF16)
    HTO = dscr("HTO", [128, KC * OWN], BF16)
    OAT = dscr("OAT", [128, (c.AW // 128) * OWN], BF16)
    OBT = dscr("OBT", [128, (c.BW // 128) * OWN], BF16)
    X1 = dscr("X1", [OWN, D], F32)
    X2 = dscr("X2", [OWN, D], F32)

    class _Stop(Exception):
        pass

    def chk(n):
        if stop is not None and stop == n:
            raise _Stop()

    es = ExitStack()
    ARENA_BYTES = 206 * 1024
    arena = es.enter_context(nc.sbuf_tensor("arena", [128, ARENA_BYTES // 2], BF16))
    psum = es.enter_context(nc.psum_tensor("psum", [128, 4096], F32))
    Sd = Sched(nc, es)
    K.S = Sd

    class Arena:
        def __init__(s):
            s.top = 0

        def alloc(s, nbytes):
            off = (s.top + 63) // 64 * 64
            s.top = off + nbytes
            assert s.top <= ARENA_BYTES, ("SBUF arena overflow", s.top)
            return off

    A = Arena()

    def view(off, dt, shape):
        n = 1
        for d_ in shape:
            n *= d_
        esz = 2 if dt == BF16 else 4
        a = arena[:, off // 2:(off + n * esz) // 2]
        if dt != BF16:
            a = a.bitcast(dt)
        if len(shape) == 2:
            a = a.rearrange("p (a b) -> p a b", a=shape[0])
        elif len(shape) == 3:
            a = a.rearrange("p (a b c) -> p a b c", a=shape[0], b=shape[1])
        return a

    def alloc(dt, shape):
        n = 1
        for d_ in shape:
            n *= d_
        esz = 2 if dt == BF16 else 4
        return view(A.alloc(n * esz), dt, shape)

    def bank(b, dt=F32):
        a = psum[:, b * 512:(b + 1) * 512]
        if dt == BF16:
            a = a.bitcast(BF16)
        return a

    PB = [T() for _ in range(8)]

    def mm(out, lhsT, rhs, start, stop, rd, wr, inc):
        Sd.op("pe", lambda e: e.matmul(out, lhsT, rhs, start=start, stop=stop), rd, wr, inc)

    def tr(out, in_, rd, wr, inc):
        Sd.op("pe", lambda e: e.transpose(out, in_, ident), rd, wr, inc)

    def act(out, in_, func, rd, wr, bias=0.0, scale=1.0, accum=None):
        if accum is None:
            Sd.op("act", lambda e: e.activation(out, in_, func, bias=bias, scale=scale), rd, wr)
        else:
            Sd.op("act", lambda e: e.activation(out, in_, func, bias=bias, scale=scale, accum_out=accum), rd, wr)

    def ts(out, in0, s1, s2, op0, op1, rd, wr, eng="dve"):
        if s2 is None:
            Sd.op(eng, lambda e: e.tensor_scalar(out, in0, s1, None, op0), rd, wr)
        else:
            Sd.op(eng, lambda e: e.tensor_scalar(out, in0, s1, s2, op0, op1), rd, wr)

    def tt(out, in0, in1, op, rd, wr, eng="dve"):
        Sd.op(eng, lambda e: e.tensor_tensor(out, in0, in1, op), rd, wr)

    def stt(out, in0, scalar, in1, op0, op1, rd, wr, eng="dve"):
        Sd.op(eng, lambda e: e.scalar_tensor_tensor(out, in0, scalar, in1, op0, op1), rd, wr)

    def cp(out, in_, rd, wr, eng="dve"):
        Sd.op(eng, lambda e: e.tensor_copy(out, in_), rd, wr)

    def memset(ap, val, wr, eng="dve"):
        Sd.op(eng, lambda e: e.memset(ap, val), (), wr)

    def recip(out, in_, rd, wr):
        Sd.op("dve", lambda e: e.reciprocal(out, in_), rd, wr)

    def bcast_ap(handle, off, n):
        return bass.AP(handle, off, [[0, 128], [1, n]])

    NW = 3
    WBYTES = max(KC * 256 * 2, (c.SLAB // 128) * 512 * 2, (KC // 2) * 512 * 2)
    wslots = []
    for i in range(NW):
        off = A.alloc(WBYTES)
        wslots.append((off, T()))
    wring = Ring(wslots)

    ident = alloc(BF16, [128])
    T_const = T()
    invf = alloc(F32, [1])
    sgn = alloc(F32, [1])
    validt = alloc(F32, [S // 128])
    lamt = alloc(F32, [1])
    sub08 = alloc(F32, [256])
    gpc = {n: alloc(F32, [KC]) for n in ("g_mix_pre", "g_mem_pre", "g_mem_kv", "g_mlp_pre")}
    KTm = alloc(BF16, [c.HM, c.MEM])
    VMa = alloc(BF16, [c.MEM // 128, c.HM, 129])
    T_mem = T()
    small = alloc(F32, [64])
    T_small = T()
    G_TOP = A.top

    def phase0():
        Sd.dma("pool", ident, ident_in.ap(), (), [T_const], semT=T_const)
        Sd.dma("sp", invf, invf_in.ap(), (), [T_const], semT=T_const)
        Sd.dma("sp", sgn, sgn_in.ap(), (), [T_const], semT=T_const)
        Sd.dma("sp", validt, valid.ap(), (), [T_const], semT=T_const)
        for n in gpc:
            Sd.dma("sp", gpc[n], gpc_in[n].ap(), (), [T_const], semT=T_const)
        Sd.dma("sp", sub08, bcast_ap(subln, 0, 256), (), [T_const], semT=T_const)
        tmp = alloc(F32, [4, 128])
        Tt = T()
        for i, n in enumerate(("lq1", "lk1", "lq2", "lk2")):
            Sd.dma("sp", tmp[:, i, :], bcast_ap(lam_in[n], 0, 128), (), [Tt], semT=Tt)
        tt(tmp[:, 0, :], tmp[:, 0, :], tmp[:, 1, :], ALU.mult, [Tt], [Tt])
        tt(tmp[:, 2, :], tmp[:, 2, :], tmp[:, 3, :], ALU.mult, [Tt], [Tt])
        Sd.op("dve", lambda e: e.reduce_sum(small[:, 0:1], tmp[:, 0, :], AX.X), [Tt], [T_small])
        Sd.op("dve", lambda e: e.reduce_sum(small[:, 1:2], tmp[:, 2, :], AX.X), [Tt], [T_small])
        act(small[:, 2:4], small[:, 0:2], AF.Exp, [T_small], [T_small])
        tt(lamt, small[:, 2:3], small[:, 3:4], ALU.subtract, [T_small], [T_const])
        ts(lamt, lamt, LAM_INIT, None, ALU.add, None, [T_const], [T_const])
        ts(sub08, sub08, 1.0 - LAM_INIT, None, ALU.mult, None, [T_const], [T_const])

    def rstd_from_ss(dst, ss, n, rd, wr):
        act(dst, ss, AF.Sqrt, rd, wr, bias=EPS, scale=1.0 / n)
        recip(dst, dst, wr, wr)

    tpr = Ring([0, 1])

    def norm_T(xt, Txt, hb, Thb, gname, hT, ThT, tok_off, sc0):
        ss = small[:, sc0:sc0 + 1]
        rs = small[:, sc0 + 1:sc0 + 2]
        memset(ss, 0.0, [T_small])
        chk(19)
        act(hb, xt, AF.Square, [Txt], [Thb, T_small], accum=ss)
        chk(20)
        rstd_from_ss(rs, ss, D, [T_small], [T_small])
        ts(hb, xt, rs, None, ALU.mult, None, [Txt, T_small], [Thb], eng=POOL_ENG)
        chk(21)
        g = gpc[gname]
        for k0 in range(0, KC, 8):
            b = tpr.next()
            nk = min(8, KC - k0)
            for j in range(nk):
                kc = k0 + j
                tr(bank(b, BF16)[:, j * 128:(j + 1) * 128], hb[:, kc * 128:(kc + 1) * 128], [Thb, T_const], [PB[b]],
                   inc=(j == nk - 1))
            chk(22)
            for j in range(nk):
                kc = k0 + j
                src = bank(b, BF16)[:, j * 128:(j + 1) * 128]
                dst = hT[:, kc, tok_off:tok_off + 128]
                if kc % 2 == 0 and EVAC_MODE != 1 or EVAC_MODE == 2:
                    act(dst, src, EVAC_FUNC, [PB[b], T_const], [ThT], scale=g[:, kc:kc + 1])
                else:
                    ts(dst, src, g[:, kc:kc + 1], None, ALU.mult, None, [PB[b], T_const], [ThT])
            chk(23)

    def stream(pieces, PF=2, ring=None):
        slots = [None] * len(pieces)
        if ring is None:
            ring = wring
        else:
            PF = len(ring.items) - 1

        def issue(i):
            sl = ring.next()
            pieces[i][0](sl)
            slots[i] = sl

        for i in range(min(PF, len(pieces))):
            issue(i)
        for i in range(len(pieces)):
            if i + PF < len(pieces):
                issue(i + PF)
            pieces[i][1](slots[i])

    def wload_fm(slot, w_handle, nkc, c0, ncols, sub_off=0):
        off, Tw = slot
        dst = view(off + sub_off, BF16, [nkc, ncols])
        src = w_handle.ap()[:, c0:c0 + ncols].rearrange("(kc p) c -> p kc c", p=128)
        Sd.dma("pool", dst, src, (), [Tw], semT=Tw)
        return dst

    def wload_rows(slot, w_handle, r0, nkc, c0, ncols):
        off, Tw = slot
        dst = view(off, BF16, [nkc, ncols])
        src = w_handle.ap()[r0:r0 + nkc * 128, c0:c0 + ncols].rearrange("(kc p) c -> p kc c", p=128)
        Sd.dma("pool", dst, src, (), [Tw], semT=Tw)
        return dst

    def phase_AB():
        xs = [(alloc(F32, [D]), T()) for _ in range(2)]
        hbs = [(alloc(BF16, [D]), T()) for _ in range(2)]
        hT = alloc(BF16, [KC, 1024])
        ThT = T(multi=True)
        cos2 = alloc(F32, [1024])
        sin2 = alloc(F32, [1024])
        T_rope = T()
        posi = alloc(I32, [1024])
        posf = alloc(F32, [1024])
        T_pos = T()
        rt_off = [A.alloc(4096) for _ in range(2)]
        ropet = [(view(o_, F32, [512]), view(o_ + 2048, F32, [512]), T()) for o_ in rt_off]
        ang, T_ang = view(rt_off[0], F32, [1024]), ropet[0][2]
        wtmp, T_wtmp = view(rt_off[1], F32, [1024]), ropet[1][2]
        obfs = [(alloc(BF16, [512]), T()) for _ in range(3)]
        vbfs = [(alloc(BF16, [256]), T()) for _ in range(3)]
        r_xs, r_hb, r_rt, r_ob, r_vb = Ring(xs), Ring(hbs), Ring(ropet), Ring(obfs), Ring(vbfs)
        pbr = Ring([2, 3, 4, 5, 6, 7])

        def need_tiles(kind, g):
            if kind == "q":
                return set(range(8))
            if kind in ("kb", "vb"):
                return set(range(32))
            o = [1, 2, 8][g]
            return set(range(32 - o, 32)) | set(range(0, 8 + o))

        for tg in range(S // 1024):
            for t8 in range(8):
                xt, Txt = r_xs.next()
                hb, Thb = r_hb.next()
                tok0 = tg * 1024 + t8 * 128
                Sd.dma("sp", xt, xr.ap()[tok0:tok0 + 128, :], (), [Txt], semT=Txt)
                norm_T(xt, Txt, hb, Thb, "g_mix_pre", hT, ThT, t8 * 128, 4 + 2 * (t8 % 2))
            chk(10)
            if tg == 0:
                Sd.dma("sp", HTO.ap(), hT.rearrange("p a b -> p (a b)"), [ThT], (), semT=ThT)
            chk(11)
            Sd.dma("sp", posi, bcast_ap(pos, tg * 1024, 1024), (), [T_pos], semT=T_pos)
            cp(posf, posi, [T_pos], [T_pos])
            ts(ang, posf, invf, None, ALU.mult, None, [T_pos, T_const], [T_ang])
            ts(posf, ang, 1.0 / (2 * PI), None, ALU.mult, None, [T_ang], [T_pos])
            cp(posi, posf, [T_pos], [T_pos])
            cp(posf, posi, [T_pos], [T_pos])
            stt(ang, posf, -2 * PI, ang, ALU.mult, ALU.add, [T_pos, T_ang], [T_ang])
            for (dst, shift, use_sgn) in ((sin2, 0.0, True), (cos2, 0.5 * PI, False)):
                ts(dst, ang, shift, None, ALU.add, None, [T_ang], [T_rope])
                ts(posf, dst, -PI, 2 * PI, ALU.is_lt, ALU.mult, [T_rope], [T_pos])
                ts(wtmp, dst, PI, -2 * PI, ALU.is_gt, ALU.mult, [T_rope], [T_wtmp])
                tt(dst, dst, posf, ALU.add, [T_rope, T_pos], [T_rope])
                tt(dst, dst, wtmp, ALU.add, [T_rope, T_wtmp], [T_rope])
                ts(dst, dst, -PI_SAFE, PI_SAFE, ALU.max, ALU.min, [T_rope], [T_rope])
                act(dst, dst, AF.Sin, [T_rope], [T_rope])
                if use_sgn:
                    ts(dst, dst, sgn, None, ALU.mult, None, [T_rope, T_const], [T_rope])

            chk(12)
            pieces = []

            def add_qk(kind, g, col0, head0):
                tl = need_tiles(kind, g)
                tbs = []
                for tb in (0, 1):
                    need = [t for t in range(4) if (tg * 8 + tb * 4 + t) in tl]
                    if need:
                        assert need == list(range(need[0], need[-1] + 1))
                        tbs.append((tb * 512 + need[0] * 128, len(need) * 128))
                if not tbs:
                    return
                dstT = QT if kind == "q" else KT

                def load(sl, col0=col0):
                    wload_fm(sl, w_in, KC, col0, 256)

                def comp(sl, tbs=tbs, head0=head0, dstT=dstT, kind=kind):
                    off, Tw = sl
                    w = view(off, BF16, [KC, 256])
                    for hh in range(2):
                        for (o0, n_) in tbs:
                            b = pbr.next()
                            ps = bank(b)[:, 0:n_]
                            for kc in range(KC):
                                mm(ps, w[:, kc, hh * 128:(hh + 1) * 128], hT[:, kc, o0:o0 + n_],
                                   kc == 0, kc == KC - 1, [Tw, ThT], [PB[b]], kc == KC - 1)
                            tA, tB, Trt = r_rt.next()
                            ob, Tob = r_ob.next()
                            tA, tB, ob = tA[:, 0:n_], tB[:, 0:n_], ob[:, 0:n_]
                            cs = cos2[:, o0:o0 + n_]
                            sn = sin2[:, o0:o0 + n_]
                            tt(tA, ps, cs, ALU.mult, [PB[b], T_rope], [Trt])
                            tt(tB[0:64, :], ps[64:128, :], sn[64:128, :], ALU.mult, [PB[b], T_rope], [Trt])
                            tt(tB[64:128, :], ps[0:64, :], sn[0:64, :], ALU.mult, [PB[b], T_rope], [Trt])
                            tt(ob, tA, tB, ALU.add, [Trt], [Tob])
                            if kind == "q":
                                dst = dstT.ap()[head0 + hh][:, o0:o0 + n_]
                            else:
                                t0 = tg * 1024 + o0
                                dst = dstT.ap()[head0 + hh][:, t0:t0 + n_]
                            Sd.dma("sp", dst, ob, [Tob], (), semT=Tob)

                pieces.append((load, comp))

            def add_v(kind, g, col0, head0):
                tl = need_tiles(kind, g)
                t8s = [t8 for t8 in range(8) if (tg * 8 + t8) in tl]
                if not t8s:
                    return

                def load(sl, col0=col0):
                    wload_fm(sl, w_in, KC, col0, 256)

                def comp(sl, t8s=t8s, head0=head0, kind=kind):
                    off, Tw = sl
                    w = view(off, BF16, [KC, 256])
                    for t8 in t8s:
                        b = pbr.next()
                        for kc in range(KC):
                            mm(bank(b)[:, 0:256], hT[:, kc, t8 * 128:(t8 + 1) * 128], w[:, kc, :],
                               kc == 0, kc == KC - 1, [Tw, ThT], [PB[b]], kc == KC - 1)
                        vb, Tvb = r_vb.next()
                        tok0 = tg * 1024 + t8 * 128
                        ch = tok0 // 128
                        if kind == "va":
                            ts(vb, bank(b)[:, 0:256], validt[:, ch:ch + 1], None, ALU.mult, None, [PB[b], T_const], [Tvb])
                            Sd.dma("sp", VA.ap()[head0][tok0:tok0 + 128, :], vb[:, 0:128], [Tvb], (), semT=Tvb)
                            Sd.dma("sp", VA.ap()[head0 + 1][tok0:tok0 + 128, :], vb[:, 128:256], [Tvb], (), semT=Tvb)
                        else:
                            act(vb, bank(b)[:, 0:256], AF.Copy, [PB[b]], [Tvb])
                            Sd.dma("sp", VB.ap()[head0][tok0:tok0 + 128, :], vb, [Tvb], (), semT=Tvb)

                pieces.append((load, comp))

            for g in range(3):
                for hp in range(c.HA // 2):
                    h0 = g * c.HA + hp * 2
                    add_qk("q", g, c.o_qa + h0 * 128, h0)
                    add_qk("ka", g, c.o_ka + h0 * 128, h0)
                    add_v("va", g, c.o_va + h0 * 128, h0)
            for h in range(c.HB):
                if stop == 13:
                    break
                add_qk("q", 9, c.o_qb + h * 256, 3 * c.HA + 2 * h)
                add_qk("kb", 9, c.o_kb + h * 256, 3 * c.HA + 2 * h)
                add_v("vb", 9, c.o_vb + h * 256, h)
            stream(pieces)
            chk(13)
            chk(14)

    def phase_C():
        scale = 128.0 ** -0.5
        mark0 = A.top
        p_ring = Ring([(alloc(BF16, [512]), T()) for _ in range(8)])
        st_ring = Ring([4, 5, 6])
        tp_ring = Ring([7])
        SKEW = 2
        accs = [0, 1, 2, 3]
        ostage = Ring([(alloc(BF16, [256]), T()) for _ in range(10)])
        oT_stage = Ring([(alloc(BF16, [2, 128]), T()) for _ in range(4)])
        accsb = Ring([(alloc(F32, [260]), T()) for _ in range(8)])
        pending = []

        def flush():
            while pending:
                pending.pop(0)()

        def evac_accs(E):
            res = []
            for j in range(4):
                sb, Tsb = accsb.next()
                if j % 2 == 0:
                    act(sb[:, 0:E + 1], bank(accs[j])[:, 0:E + 1], AF.Copy, [PB[accs[j]]], [Tsb])
                else:
                    cp(sb[:, 0:E + 1], bank(accs[j])[:, 0:E + 1], [PB[accs[j]]], [Tsb])
                res.append((sb, Tsb))
            return res

        mark = A.top

        masks = alloc(BF16, [NM, 512])
        Sd.dma("pool", masks, masks_in.ap().rearrange("m p f -> p m f"), (), [T_const], semT=T_const)
        offs = [128, 256, 1024]
        spans = [1024 + 2 * o for o in offs]
        sets = []
        for _ in range(2):
            d = dict(
                q=alloc(BF16, [3, OWN]), Tq=T(),
                k=[alloc(BF16, [spans[g]]) for g in range(3)], Tk=T(),
                v=[alloc(BF16, [spans[g] // 128, 129]) for g in range(3)], Tv=T(),
            )
            sets.append(d)

        def load_dil(h, d):
            for g in range(3):
                hq = g * c.HA + h
                o = offs[g]
                Sd.dma("sp", d["q"][:, g, :], QT.ap()[hq], (), [d["Tq"]], semT=d["Tq"])
                Sd.dma("sp", d["k"][g][:, 0:o], KT.ap()[hq][:, S - o:S], (), [d["Tk"]], semT=d["Tk"])
                Sd.dma("sp", d["k"][g][:, o:o + 1024 + o], KT.ap()[hq][:, 0:1024 + o], (), [d["Tk"]], semT=d["Tk"])
                nw = o // 128
                Sd.dma("sp", d["v"][g][:, 0:nw, 0:128],
                       VA.ap()[hq][S - o:S, :].rearrange("(c p) e -> p c e", p=128), (), [d["Tv"]], semT=d["Tv"])
                Sd.dma("sp", d["v"][g][:, nw:nw + 8 + nw, 0:128],
                       VA.ap()[hq][0:1024 + o, :].rearrange("(c p) e -> p c e", p=128), (), [d["Tv"]], semT=d["Tv"])
                cp(d["v"][g][:, 0:nw, 128], validt[:, S // 128 - nw:S // 128], [T_const], [d["Tv"]])
                cp(d["v"][g][:, nw:nw + 8 + nw, 128], validt[:, 0:8 + nw], [T_const], [d["Tv"]])

        def finish_dil(h, qb):
            staged = evac_accs(128)
            obs = []
            for j in range(4):
                sb, Tsb = staged[j]
                rc = small[:, 16 + j:17 + j]
                recip(rc, sb[:, 128:129], [Tsb], [T_small])
                ob, Tob = ostage.next()
                ts(ob[:, 0:128], sb[:, 0:128], rc, None, ALU.mult, None, [Tsb, T_small], [Tob])
                obs.append((ob, Tob))

            def part2():
                for j in range(4):
                    ob, Tob = obs[j]
                    b = tp_ring.next()
                    tr(bank(b, BF16)[:, 0:128], ob[:, 0:128], [Tob, T_const], [PB[b]], True)
                    oT, ToT = oT_stage.next()
                    cp(oT[:, 0, :], bank(b, BF16)[:, 0:128], [PB[b]], [ToT])
                    t0 = qb * 512 + j * 128
                    Sd.dma("sp", OAT.ap()[:, h * OWN + t0:h * OWN + t0 + 128], oT[:, 0, :], [ToT], (), semT=ToT)

            pending.append(part2)

        load_dil(0, sets[0])
        for h in range(c.HA):
            d = sets[h % 2]
            if h + 1 < c.HA:
                load_dil(h + 1, sets[(h + 1) % 2])
            for qb in range(2):
                for g in range(3):
                    gch = []
                    for (delta, mi) in plan[g]:
                        li = qb * 512 + delta + offs[g]
                        gch.append((d["k"][g][:, li:li + 128], d["Tk"], d["v"][g][:, li // 128, :], d["Tv"],
                                    masks[:, mi, :]))
                    attend_grp(d["q"][:, g, qb * 512:(qb + 1) * 512], d["Tq"], gch, 128, accs, st_ring, p_ring,
                               scale, g == 0, g == 2, skew=SKEW, hook=(flush if g == 0 else None))
                finish_dil(h, qb)
        flush()
        Sd.barrier(new_epoch=True)
        A.top = mark
        chk(2)

        NCH = S // 128
        sets = []
        for _ in range(2):
            d = dict(q=alloc(BF16, [2, OWN]), Tq=T(), k=alloc(BF16, [2, S]), Tk=T(),
                     v=alloc(BF16, [NCH, 257]), Tv=T())
            sets.append(d)
        for d in sets:
            memset(d["v"][:, :, 256:257], 1.0, [d["Tv"]])
        on1 = alloc(F32, [4, 256])
        T_on1 = [T() for _ in range(4)]
        ods = Ring([(alloc(F32, [256]), T()) for _ in range(2)])

        def load_diff(h, d):
            for cc in range(2):
                hq = 3 * c.HA + 2 * h + cc
                Sd.dma("sp", d["q"][:, cc, :], QT.ap()[hq], (), [d["Tq"]], semT=d["Tq"])
                Sd.dma("sp", d["k"][:, cc, :], KT.ap()[hq], (), [d["Tk"]], semT=d["Tk"])
            Sd.dma("sp", d["v"][:, :, 0:256], VB.ap()[h].rearrange("(c p) e -> p c e", p=128), (), [d["Tv"]],
                   semT=d["Tv"])

        def finish_diff(h, qb, cc):
            staged = evac_accs(256)
            obs = []
            for j in range(4):
                sb, Tsb = staged[j]
                rc = small[:, 16 + j:17 + j]
                recip(rc, sb[:, 256:257], [Tsb], [T_small])
                if cc == 0:
                    ts(on1[:, j, :], sb[:, 0:256], rc, None, ALU.mult, None, [Tsb, T_small], [T_on1[j]])
                    continue
                od, T_od = ods.next()
                tt(rc, rc, lamt, ALU.mult, [T_small, T_const], [T_small])
                ts(od, sb[:, 0:256], rc, None, ALU.mult, None, [Tsb, T_small], [T_od])
                tt(od, on1[:, j, :], od, ALU.subtract, [T_on1[j], T_od], [T_od])
                ss = small[:, 24 + j:25 + j]
                rs = small[:, 28 + j:29 + j]
                ob, Tob = ostage.next()
                memset(ss, 0.0, [T_small])
                act(ob, od, AF.Square, [T_od], [Tob, T_small], accum=ss)
                rstd_from_ss(rs, ss, 256, [T_small], [T_small])
                stt(ob, od, rs, sub08, ALU.mult, ALU.mult, [T_od, T_small, T_const], [Tob])
                obs.append((ob, Tob))
            if cc == 0:
                return

            def part2():
                for j in range(4):
                    ob, Tob = obs[j]
                    oT, ToT = oT_stage.next()
                    for e2 in range(2):
                        b = tp_ring.next()
                        tr(bank(b, BF16)[:, 0:128], ob[:, e2 * 128:(e2 + 1) * 128], [Tob, T_const], [PB[b]], True)
                        cp(oT[:, e2, :], bank(b, BF16)[:, 0:128], [PB[b]], [ToT])
                    t0 = qb * 512 + j * 128
                    for e2 in range(2):
                        fc = 2 * h + e2
                        Sd.dma("sp", OBT.ap()[:, fc * OWN + t0:fc * OWN + t0 + 128], oT[:, e2, :], [ToT], (),
                               semT=ToT)

            pending.append(part2)

        load_diff(0, sets[0])
        for h in range(c.HB):
            d = sets[h % 2]
            if h + 1 < c.HB:
                load_diff(h + 1, sets[(h + 1) % 2])
            for qb in range(2):
                for cc in range(2):
                    chunks = [(d["k"][:, cc, ci * 128:(ci + 1) * 128], d["Tk"], d["v"][:, ci, :], d["Tv"], None)
                              for ci in range(NCH)]
                    attend_grp(d["q"][:, cc, qb * 512:(qb + 1) * 512], d["Tq"], chunks, 256, accs, st_ring, p_ring,
                               scale, True, True, skew=SKEW, hook=flush)
                    finish_diff(h, qb, cc)
        flush()
        Sd.barrier()
        A.top = mark0

    def attend_grp(qT, Tq, chunks, E, accs, st_ring, p_ring, scale, first, last, skew=1, hook=None):
        n = len(chunks)
        stb = [None] * n

        def qk(ci):
            b = st_ring.next()
            stb[ci] = b
            mm(bank(b), chunks[ci][0], qT, True, True, [chunks[ci][1], Tq], [PB[b]], True)

        for ci in range(min(skew, n)):
            qk(ci)
        for ci in range(n):
            if ci + skew < n:
                qk(ci + skew)
            kt, Tk, va, Tv, mk = chunks[ci]
            b = stb[ci]
            pt, Tp = p_ring.next()
            act(pt, bank(b), AF.Exp, [PB[b]], [Tp], scale=scale)
            if mk is not None:
                tt(pt, pt, mk, ALU.mult, [Tp, T_const], [Tp])
            for j in range(4):
                mm(bank(accs[j])[:, 0:E + 1], pt[:, j * 128:(j + 1) * 128], va, first and ci == 0,
                   last and ci == n - 1, [Tp, Tv], [PB[accs[j]]], j == 3)
            if hook is not None and ci == min(5, n - 1):
                hook()

    def phase_M():
        mark = A.top
        xs = [(alloc(F32, [D]), T()) for _ in range(2)]
        hbs = [(alloc(BF16, [D]), T()) for _ in range(2)]
        mT = alloc(BF16, [KC, c.MEM])
        TmT = T(multi=True)
        for t in range(c.MEM // 128):
            xt, Txt = xs[t % 2]
            hb, Thb = hbs[t % 2]
            Sd.dma("sp", xt, memx.ap()[t * 128:(t + 1) * 128, :], (), [Txt], semT=Txt)
            norm_T(xt, Txt, hb, Thb, "g_mem_kv", mT, TmT, t * 128, 4 + 2 * t)
        memset(VMa[:, :, :, 128:129], 1.0, [T_mem])
        pbr = Ring([2, 3, 4, 5])
        pieces = []
        for hm in range(c.HM):
            def loadk(sl, hm=hm):
                wload_fm(sl, w_mkv, KC, hm * 128, 128)

            def compk(sl, hm=hm):
                off, Tw = sl
                w = view(off, BF16, [KC, 128])
                b = pbr.next()
                for kc in range(KC):
                    mm(bank(b)[:, 0:c.MEM], w[:, kc, :], mT[:, kc, :], kc == 0, kc == KC - 1, [Tw, TmT], [PB[b]],
                       kc == KC - 1)
                act(KTm[:, hm, :], bank(b)[:, 0:c.MEM], AF.Copy, [PB[b]], [T_mem])

            def loadv(sl, hm=hm):
                wload_fm(sl, w_mkv, KC, c.MW + hm * 128, 128)

            def compv(sl, hm=hm):
                off, Tw = sl
                w = view(off, BF16, [KC, 128])
                for t in range(c.MEM // 128):
                    b = pbr.next()
                    for kc in range(KC):
                        mm(bank(b)[:, 0:128], mT[:, kc, t * 128:(t + 1) * 128], w[:, kc, :], kc == 0, kc == KC - 1,
                           [Tw, TmT], [PB[b]], kc == KC - 1)
                    act(VMa[:, t, hm, 0:128], bank(b)[:, 0:128], AF.Copy, [PB[b]], [T_mem])

            pieces.append((loadk, compk))
            pieces.append((loadv, compv))
        stream(pieces)
        Sd.barrier()
        A.top = mark

    def norm_residual(ybuf, Ty, src_dram, dst_dram, gpost_name, gpre_name, hTn, ThTn, tokbase):
        gpost = alloc(F32, [D])
        xts = [(alloc(F32, [D]), T()) for _ in range(1)]
        hbs2 = [(alloc(BF16, [D]), T()) for _ in range(1)]
        Tg = T()
        Sd.dma("sp", gpost, bcast_ap(gv[gpost_name], 0, D), (), [Tg], semT=Tg)
        for t4 in range(4):
            xt, Txt = xts[t4 % len(xts)]
            hb, Thb = hbs2[t4 % len(hbs2)]
            tok0 = tokbase + t4 * 128
            Sd.dma("sp", xt, src_dram.ap()[tok0:tok0 + 128, :], (), [Txt], semT=Txt)
            ss = small[:, 32 + 2 * (t4 % 2):33 + 2 * (t4 % 2)]
            rs = small[:, 33 + 2 * (t4 % 2):34 + 2 * (t4 % 2)]
            y = ybuf[:, t4, :]
            memset(ss, 0.0, [T_small])
            act(hb, y, AF.Square, [Ty], [Thb, T_small], accum=ss)
            rstd_from_ss(rs, ss, D, [T_small], [T_small])
            stt(y, y, rs, gpost, ALU.mult, ALU.mult, [Ty, T_small, Tg], [Ty])
            tt(xt, xt, y, ALU.add, [Txt, Ty], [Txt], eng=POOL_ENG)
            Sd.dma("sp", dst_dram.ap()[tok0:tok0 + 128, :], xt, [Txt], (), semT=Txt)
            if gpre_name is not None:
                norm_T(xt, Txt, hb, Thb, gpre_name, hTn, ThTn, t4 * 128, 36 + 2 * (t4 % 2))

    MLP_T = [T() for _ in range(MLP_EXTRA)]

    def post_half(hf):
        tokbase = hf * 512
        mark = A.top
        scale = 128.0 ** -0.5
        NA, NB = c.AW // 128, c.BW // 128
        hTreg = alloc(BF16, [KC, 512])
        yoff = A.alloc(4 * D * 4)
        ybuf = view(yoff, F32, [4, D])
        Ty = T(multi=True)
        Xbase = A.top

        hTo = hTreg
        T_in = T()
        mT = alloc(BF16, [KC, 512])
        TmT = T(multi=True)
        oaT = alloc(BF16, [NA, 512])
        o2 = NB * 512 * 2
        soff = yoff if o2 + 4 * 2048 <= 4 * D * 4 else A.alloc(o2 + 4 * 2048)
        obT = view(soff, BF16, [NB, 512])
        sg = [[view(soff + o2 + (k * 2 + j) * 2048, F32, [512]) for j in range(2)] for k in range(2)]
        Tsg = [[T() for j in range(2)] for k in range(2)]
        Sd.dma("sp", hTo, HTO.ap().rearrange("p (a b) -> p a b", a=KC)[:, :, tokbase:tokbase + 512], (), [T_in],
               semT=T_in)
        Sd.dma("sp", oaT, OAT.ap().rearrange("p (a b) -> p a b", a=NA)[:, :, tokbase:tokbase + 512], (), [T_in],
               semT=T_in)
        Sd.dma("sp", obT, OBT.ap().rearrange("p (a b) -> p a b", a=NB)[:, :, tokbase:tokbase + 512], (), [T_in],
               semT=T_in)
        pieces = []
        for cc in range(D // 256):
            def load_g(sl, cc=cc, k=0):
                wload_fm(sl, w_in, KC, (c.o_ga if k == 0 else c.o_gb) + cc * 256, 256)

            def comp_g(sl, cc=cc, k=0):
                off, Tw = sl
                w = view(off, BF16, [KC, 256])
                for j in range(2):
                    b = 2 * k + j
                    for kc in range(KC):
                        mm(bank(b), w[:, kc, j * 128:(j + 1) * 128], hTo[:, kc, :], kc == 0, kc == KC - 1,
                           [Tw, T_in], [PB[b]], kc == KC - 1)
                    act(sg[k][j], bank(b), AF.Sigmoid, [PB[b]], [Tsg[k][j]])

            def load_ab(sl, cc=cc):
                wload_fm(sl, w_a, NA, cc * 256, 256)
                wload_fm(sl, w_b, NB, cc * 256, 256, sub_off=NA * 256 * 2)

            def comp_ab(sl, cc=cc):
                off_a, Tab = sl
                wa = view(off_a, BF16, [NA, 256])
                wb = view(off_a + NA * 256 * 2, BF16, [NB, 256])
                for j in range(2):
                    cs = slice(j * 128, (j + 1) * 128)
                    b2, b3 = 4 + j, 6 + j
                    for kc in range(NA):
                        mm(bank(b2), wa[:, kc, cs], oaT[:, kc, :], kc == 0, kc == NA - 1, [Tab, T_in], [PB[b2]],
                           kc == NA - 1)
                    for kc in range(NB):
                        mm(bank(b3), wb[:, kc, cs], obT[:, kc, :], kc == 0, kc == NB - 1, [Tab, T_in], [PB[b3]],
                           kc == NB - 1)
                    tt(sg[0][j], sg[0][j], bank(b2), ALU.mult, [Tsg[0][j], PB[b2]], [Tsg[0][j]])
                    tt(sg[1][j], sg[1][j], bank(b3), ALU.mult, [Tsg[1][j], PB[b3]], [Tsg[1][j]])
                    tt(mT[:, cc * 2 + j, :], sg[0][j], sg[1][j], ALU.add, [Tsg[0][j], Tsg[1][j]], [TmT])

            pieces.append((load_g, comp_g))
            pieces.append((lambda sl, cc=cc: load_g(sl, cc, 1), lambda sl, cc=cc: comp_g(sl, cc, 1)))
            pieces.append((load_ab, comp_ab))
        stream(pieces)
        Sd.barrier()

        def proj_tm(aT, TaT, nkc_total, w_handle, row0):
            nsp = 2 if nkc_total >= 2 and nkc_total * 512 * 2 > WBYTES else 1
            nk = nkc_total // nsp
            pcs = []
            pbs = Ring([(0, 1, 2, 3), (4, 5, 6, 7)])
            state = {}
            for ct in range(D // 512):
                for kh in range(nsp):
                    def load(sl, ct=ct, kh=kh):
                        wload_rows(sl, w_handle, row0 + kh * nk * 128, nk, ct * 512, 512)

                    def comp(sl, ct=ct, kh=kh):
                        off, Tw = sl
                        w = view(off, BF16, [nk, 512])
                        if kh == 0:
                            state["b"] = pbs.next()
                        bs = state["b"]
                        for t4 in range(4):
                            for kl in range(nk):
                                kc = kh * nk + kl
                                mm(bank(bs[t4]), aT[:, kc, t4 * 128:(t4 + 1) * 128], w[:, kl, :],
                                   kc == 0, kc == nkc_total - 1, [TaT, Tw], [PB[bs[t4]]], kl == nk - 1)
                        if kh == nsp - 1:
                            for t4 in range(4):
                                dst = ybuf[:, t4, ct * 512:(ct + 1) * 512]
                                if t4 % 2 == 0:
                                    act(dst, bank(bs[t4]), AF.Copy, [PB[bs[t4]]], [Ty])
                                else:
                                    cp(dst, bank(bs[t4]), [PB[bs[t4]]], [Ty])

                    pcs.append((load, comp))
            stream(pcs)

        proj_tm(mT, TmT, KC, w_mix, 0)
        Sd.barrier()
        A.top = Xbase
        h2T = hTreg
        Th2 = T(multi=True)
        norm_residual(ybuf, Ty, xr, X1, "g_mix_post", "g_mem_pre", h2T, Th2, tokbase)
        Sd.barrier()
        A.top = Xbase
        QTm = alloc(BF16, [c.HM, 512])
        TQm = T()
        oTm = alloc(BF16, [c.HM, 512])
        ToTm = T(multi=True)
        p_ring = Ring([(alloc(BF16, [512]), T()) for _ in range(4)])
        ost = Ring([(alloc(BF16, [128]), T()) for _ in range(2)])
        pbr = Ring([6, 7])
        pieces = []
        for hm in range(c.HM):
            def load(sl, hm=hm):
                wload_fm(sl, w_mq, KC, hm * 128, 128)

            def comp(sl, hm=hm):
                off, Tw = sl
                w = view(off, BF16, [KC, 128])
                b = pbr.next()
                for kc in range(KC):
                    mm(bank(b), w[:, kc, :], h2T[:, kc, :], kc == 0, kc == KC - 1, [Tw, Th2], [PB[b]], kc == KC - 1)
                act(QTm[:, hm, :], bank(b), AF.Copy, [PB[b]], [TQm])

            pieces.append((load, comp))
        stream(pieces)
        st_ring = Ring([4, 5])
        accs = [0, 1, 2, 3]
        tp_ring = Ring([6, 7])
        for hm in range(c.HM):
            chunks = [(KTm[:, hm, ci * 128:(ci + 1) * 128], T_mem, VMa[:, ci, hm, :], T_mem, None)
                      for ci in range(c.MEM // 128)]
            attend_grp(QTm[:, hm, :], TQm, chunks, 128, accs, st_ring, p_ring, scale, True, True)
            for j in range(4):
                a = bank(accs[j])
                rc = small[:, 16 + j:17 + j]
                recip(rc, a[:, 128:129], [PB[accs[j]]], [T_small])
                ob, Tob = ost.next()
                ts(ob, a[:, 0:128], rc, None, ALU.mult, None, [PB[accs[j]], T_small], [Tob])
                b = tp_ring.next()
                tr(bank(b, BF16)[:, 0:128], ob, [Tob, T_const], [PB[b]], True)
                cp(oTm[:, hm, j * 128:(j + 1) * 128], bank(b, BF16)[:, 0:128], [PB[b]], [ToTm])
        Sd.barrier()
        proj_tm(oTm, ToTm, c.HM, w_mo, 0)
        Sd.barrier()
        A.top = Xbase
        h3T = hTreg
        Th3 = T(multi=True)
        norm_residual(ybuf, Ty, X1, X2, "g_mem_post", "g_mlp_pre", h3T, Th3, tokbase)
        Sd.barrier()
        A.top = Xbase
        NFC = c.SLAB // 128
        uT = alloc(BF16, [NFC, 512])
        TuT = T(multi=True)
        rtmp = Ring([(alloc(F32, [512]), T()) for _ in range(2)])
        pbu = Ring([4, 5, 6, 7])
        mlp_ring = Ring(list(wslots) + [(A.alloc(WBYTES), MLP_T[i]) for i in range(MLP_EXTRA)])
        for sl_i in range(c.FF // c.SLAB):
            ff0 = sl_i * c.SLAB
            pieces = []
            for pp in range(c.SLAB // 256):
                def load(sl, pp=pp, ff0=ff0):
                    wload_fm(sl, w_up, KC, ff0 + pp * 256, 256)

                def comp(sl, pp=pp):
                    off, Tw = sl
                    w = view(off, BF16, [KC, 256])
                    for j in range(2):
                        b = pbu.next()
                        for kc in range(KC):
                            mm(bank(b), w[:, kc, j * 128:(j + 1) * 128], h3T[:, kc, :], kc == 0, kc == KC - 1,
                               [Tw, Th3], [PB[b]], kc == KC - 1)
                        r, Tr = rtmp.next()
                        act(r, bank(b), AF.Relu, [PB[b]], [Tr])
                        tt(uT[:, pp * 2 + j, :], r, r, ALU.mult, [Tr], [TuT])

                pieces.append((load, comp))
            for ct in range(D // 512):
                def load(sl, ct=ct, ff0=ff0):
                    wload_rows(sl, w_dn, ff0, NFC, ct * 512, 512)

                def comp(sl, ct=ct, first=(sl_i == 0)):
                    off, Tw = sl
                    w = view(off, BF16, [NFC, 512])
                    for t4 in range(4):
                        b = t4
                        for fc in range(NFC):
                            mm(bank(b), uT[:, fc, t4 * 128:(t4 + 1) * 128], w[:, fc, :], fc == 0, fc == NFC - 1,
                               [TuT, Tw], [PB[b]], fc == NFC - 1)
                        dst = ybuf[:, t4, ct * 512:(ct + 1) * 512]
                        if first:
                            cp(dst, bank(b), [PB[b]], [Ty])
                        else:
                            tt(dst, dst, bank(b), ALU.add, [Ty, PB[b]], [Ty])

                pieces.append((load, comp))
            stream(pieces, ring=mlp_ring)
        Sd.barrier()
        A.top = Xbase
        norm_residual(ybuf, Ty, X2, out, "g_mlp_post", None, None, None, tokbase)
        Sd.barrier(new_epoch=True)
        A.top = mark

    try:
        phase0()
        chk(0)
        markAB = A.top
        phase_AB()
        Sd.barrier(new_epoch=True)
        chk(1)
        A.top = markAB
        phase_C()
        chk(3)
        phase_M()
        chk(4)
        for hf in range(2):
            post_half(hf)
    except _Stop:
        pass
    Sd.barrier()

    with nc.Block() as block:
        Sd.emit(block)
    es.close()
    return nc


def make_in_maps(cfg, x, mem, positions, norm_mix_pre, w_in, w_a, w_b, w_mix_out, norm_mix_post,
                 lambda_q1, lambda_k1, lambda_q2, lambda_k2, diff_subln,
                 norm_mem_pre, norm_mem_kv, w_mem_q, w_mem_kv, w_mem_o, norm_mem_post,
                 norm_mlp_pre, w_mlp_up, w_mlp_down, norm_mlp_post):
    c = cfg
    S = c.S
    f = lambda a: np.ascontiguousarray(np.asarray(a, dtype=np.float32))
    inv = (10000.0 ** (-np.arange(0, 128, 2, dtype=np.float32) / np.float32(128))).astype(np.float32)
    invf = np.concatenate([inv, inv]).reshape(128, 1).astype(np.float32)
    sgn = np.concatenate([np.ones(64), -np.ones(64)]).reshape(128, 1).astype(np.float32)
    shared = dict(
        w_in=f(w_in[0]), w_a=f(w_a[0]), w_b=f(w_b[0]), w_mix=f(w_mix_out[0]), w_mq=f(w_mem_q[0]),
        w_mkv=f(w_mem_kv[0]), w_mo=f(w_mem_o[0]), w_up=f(w_mlp_up[0]), w_dn=f(w_mlp_down[0]),
        g_mix_pre=f(norm_mix_pre), g_mix_post=f(norm_mix_post), g_mem_pre=f(norm_mem_pre),
        g_mem_kv=f(norm_mem_kv), g_mem_post=f(norm_mem_post), g_mlp_pre=f(norm_mlp_pre),
        g_mlp_post=f(norm_mlp_post), subln=f(diff_subln), lq1=f(lambda_q1), lk1=f(lambda_k1),
        lq2=f(lambda_q2), lk2=f(lambda_k2), ident=np.eye(128, dtype=np.float32), invf=invf, sgn=sgn,
        masks=build_masks(),
    )
    for n in ("g_mix_pre", "g_mem_pre", "g_mem_kv", "g_mlp_pre"):
        shared["pc_" + n] = np.ascontiguousarray(shared[n].reshape(c.KC, 128).T)
    x = np.asarray(x, dtype=np.float32)
    mem = np.asarray(mem, dtype=np.float32)
    positions = np.asarray(positions, dtype=np.int32)
    maps = []
    for core in range(NCORES):
        b, r = core // 4, core % 4
        sh = OWN * r
        xr = np.ascontiguousarray(np.roll(x[b], -sh, axis=0))
        pr = np.ascontiguousarray(np.roll(positions[b], -sh)).reshape(1, S)
        j = np.arange(S)
        real = np.where(j < 2048, sh + j, sh + j - S)
        v = ((real >= 0) & (real < S) & ((j < 2048) | (j >= 3072))).astype(np.float32)
        valid = np.ascontiguousarray(v.reshape(S // 128, 128).T)
        m = dict(shared)
        m.update(xr=xr, pos=pr, valid=valid, mem=np.ascontiguousarray(mem[b]))
        maps.append(m)
    return maps


_NC_CACHE = {}


def run(cfg, inputs, stop=None):
    key = (cfg.D, cfg.HA, cfg.HB, cfg.HM, cfg.FF, stop)
    if key not in _NC_CACHE:
        _NC_CACHE[key] = build(cfg, stop)
    nc = _NC_CACHE[key]
    maps = make_in_maps(cfg, **inputs)
    res = run_bass_kernel_spmd(nc, maps, core_ids=list(range(NCORES)))
    outp = np.zeros((2, cfg.S, cfg.D), np.float32)
    for core in range(NCORES):
        b, r = core // 4, core % 4
        outp[b, OWN * r:OWN * (r + 1), :] = res.results[core]["out"]
    return outp


def kernel(**inputs):
    return run(Cfg(), inputs)
```

```python
import math
from contextlib import ExitStack

import numpy as np
import concourse.bass as bass
import concourse.mybir as mybir
from concourse.bass_utils import run_bass_kernel_spmd

F32 = mybir.dt.float32
BF16 = mybir.dt.bfloat16
I32 = mybir.dt.int32
ALU = mybir.AluOpType
AF = mybir.ActivationFunctionType
AX = mybir.AxisListType

EPS = 1e-6
PI = math.pi
PI_SAFE = 3.141592
LAM_INIT = 0.8 - 0.6 * 1.0
DIL_CFG = ((128, 1), (512, 4), (2048, 16))
NCORES = 8
import os
EVAC_MODE = int(os.environ.get('EVAC_MODE', '1'))
POOL_ENG = os.environ.get('POOL_ENG', 'dve')
MLP_EXTRA = int(os.environ.get('MLP_EXTRA', '2'))
EVAC_FUNC = AF.Identity if os.environ.get('EVAC_ID') else AF.Copy
OWN = 1024


class Cfg:
    def __init__(s, D=4096, HA=8, HB=8, HM=4, FF=16384, S=4096, MEM=256):
        s.D, s.HA, s.HB, s.HM, s.FF, s.S, s.MEM = D, HA, HB, HM, FF, S, MEM
        s.G = 3
        s.KC = D // 128
        s.QAW = 3 * HA * 128
        s.QBW = HB * 256
        s.o_qa = 0
        s.o_ka = s.QAW
        s.o_va = 2 * s.QAW
        s.o_qb = 3 * s.QAW
        s.o_kb = s.o_qb + s.QBW
        s.o_vb = s.o_kb + s.QBW
        s.o_ga = s.o_vb + s.QBW
        s.o_gb = s.o_ga + D
        s.DIN = s.o_gb + D
        s.AW = HA * 128
        s.BW = HB * 256
        s.MW = HM * 128
        s.NQH = 3 * HA + 2 * HB
        s.SLAB = min(2048, FF)


def dil_chunk_plan():
    masks = []
    plan = []
    for g, (win, dil) in enumerate(DIL_CFG):
        half = win // 2
        lo = -((half + 127) // 128) * 128
        hi = 512 + ((half + 127) // 128) * 128
        lst = []
        interior = None
        for delta in range(lo, hi, 128):
            is_int = (delta + 127 <= half) and (delta - 511 >= -half)
            if is_int and interior is not None:
                lst.append((delta, interior))
                continue
            masks.append((g, delta))
            if is_int:
                interior = len(masks) - 1
            lst.append((delta, len(masks) - 1))
        plan.append(lst)
    return plan, masks


def build_masks():
    plan, masks = dil_chunk_plan()
    out = np.zeros((len(masks), 128, 512), np.float32)
    p = np.arange(128)[:, None]
    f = np.arange(512)[None, :]
    for i, (g, delta) in enumerate(masks):
        win, dil = DIL_CFG[g]
        d = delta + p - f
        out[i] = ((np.abs(d) <= win // 2) & (d % dil == 0)).astype(np.float32)
    return out


class T:
    __slots__ = ("w", "r", "multi", "sem", "cnt")

    def __init__(s, multi=False):
        s.w = {}
        s.r = {}
        s.multi = multi
        s.sem = None
        s.cnt = 0


ENGS = ("pe", "act", "dve", "pool", "sp")


class Sched:
    def __init__(s, nc, es):
        s.nc = nc
        s.es = es
        s.prog = {e: [] for e in ENGS}
        s.esem = {e: es.enter_context(nc.semaphore("es_" + e)) for e in ENGS}
        s.cnt = {e: 0 for e in ENGS}
        s.seen = {e: {} for e in ENGS}
        s.semname = {}
        s.nsem = 0
        s.dma_sems = []
        for e in ENGS:
            s.semname[id(s.esem[e])] = s.esem[e]

    def _need(s, eng, rd, wr):
        deps = {}

        def mg(d):
            for k, v in d.items():
                if deps.get(k, 0) < v:
                    deps[k] = v

        for t in rd:
            mg(t.w)
        for t in wr:
            mg(t.r)
            if not t.multi:
                mg(t.w)
        own = id(s.esem[eng])
        seen = s.seen[eng]
        for k, v in deps.items():
            if k == own and eng == "pe":
                continue
            if seen.get(k, 0) < v:
                seen[k] = v
                s.prog[eng].append(("w", s.semname[k], v))

    def _record(s, ev, rd, wr):
        k, v = ev
        for t in rd:
            if t.r.get(k, 0) < v:
                t.r[k] = v
        for t in wr:
            if t.w.get(k, 0) < v:
                t.w[k] = v

    def op(s, eng, fn, rd=(), wr=(), inc=True):
        s._need(eng, rd, wr)
        if inc:
            s.cnt[eng] += 1
            ev = (id(s.esem[eng]), s.cnt[eng])
        else:
            ev = (id(s.esem[eng]), s.cnt[eng] + 1)
        s.prog[eng].append(("o", fn, s.esem[eng] if inc else None))
        s._record(ev, rd, wr)

    def dma(s, q, out, in_, rd=(), wr=(), semT=None, slow=False):
        s._need(q, rd, wr)
        if semT.sem is None:
            semT.sem = {}
            semT.cnt = {}
        if q not in semT.sem:
            s.nsem += 1
            sem = s.es.enter_context(s.nc.semaphore("ds%d" % s.nsem))
            semT.sem[q] = sem
            semT.cnt[q] = 0
            s.semname[id(sem)] = sem
            s.dma_sems.append((semT, q))
        semT.cnt[q] += 16
        s.prog[q].append(("d", out, in_, semT.sem[q], slow))
        s._record((id(semT.sem[q]), semT.cnt[q]), rd, wr)

    def barrier(s, new_epoch=False):
        s._barrier()
        if new_epoch:
            for e in ENGS:
                s.esem[e] = s.es.enter_context(s.nc.semaphore("es_%s_%d" % (e, s.nsem)))
                s.nsem += 1
                s.semname[id(s.esem[e])] = s.esem[e]
                s.cnt[e] = 0

    def _barrier(s):
        for e in ENGS:
            seen = s.seen[e]
            for e2 in ENGS:
                if e2 == e:
                    continue
                k = id(s.esem[e2])
                v = s.cnt[e2]
                if v > 0 and seen.get(k, 0) < v:
                    seen[k] = v
                    s.prog[e].append(("w", s.esem[e2], v))
            for (t, q) in s.dma_sems:
                k = id(t.sem[q])
                v = t.cnt[q]
                if v > 0 and seen.get(k, 0) < v:
                    seen[k] = v
                    s.prog[e].append(("w", t.sem[q], v))

    def emit(s, block):
        def replay(eng_name):
            lst = s.prog[eng_name]

            def f(e):
                for it in lst:
                    if it[0] == "w":
                        e.wait_ge(it[1], it[2])
                    elif it[0] == "o":
                        ins = it[1](e)
                        if it[2] is not None:
                            ins.then_inc(it[2], 1)
                    elif it[4]:
                        e.dma_start(out=it[1], in_=it[2], allow_slow_non_contiguous=True).then_inc(it[3], 16)
                    else:
                        e.dma_start(out=it[1], in_=it[2]).then_inc(it[3], 16)

            return f

        block.tensor(replay("pe"))
        block.scalar(replay("act"))
        block.vector(replay("dve"))
        block.gpsimd(replay("pool"))
        block.sync(replay("sp"))


class Ring:
    def __init__(s, items):
        s.items = items
        s.i = 0

    def next(s):
        it = s.items[s.i % len(s.items)]
        s.i += 1
        return it


class KB:
    pass


def build(cfg, stop=None):
    c = cfg
    D, KC, S = c.D, c.KC, c.S
    nc = bass.Bass("TRN2", target_bir_lowering=False)
    K = KB()
    K.nc, K.c = nc, c
    plan, mask_list = dil_chunk_plan()
    NM = len(mask_list)

    def din(name, shape, dt=F32):
        return nc.dram_tensor(name, list(shape), dt, kind="ExternalInput")

    def dscr(name, shape, dt):
        return nc.dram_tensor(name, list(shape), dt, kind="Internal")

    xr = din("xr", [S, D])
    pos = din("pos", [1, S], I32)
    valid = din("valid", [128, S // 128])
    memx = din("mem", [c.MEM, D])
    w_in = din("w_in", [D, c.DIN])
    w_a = din("w_a", [c.AW, D])
    w_b = din("w_b", [c.BW, D])
    w_mix = din("w_mix", [D, D])
    w_mq = din("w_mq", [D, c.MW])
    w_mkv = din("w_mkv", [D, 2 * c.MW])
    w_mo = din("w_mo", [c.MW, D])
    w_up = din("w_up", [D, c.FF])
    w_dn = din("w_dn", [c.FF, D])
    gnames = ["g_mix_pre", "g_mix_post", "g_mem_pre", "g_mem_kv", "g_mem_post", "g_mlp_pre", "g_mlp_post"]
    gv = {n: din(n, [1, D]) for n in gnames}
    gpc_in = {n: din("pc_" + n, [128, KC]) for n in ("g_mix_pre", "g_mem_pre", "g_mem_kv", "g_mlp_pre")}
    subln = din("subln", [1, 256])
    lam_in = {n: din(n, [1, 128]) for n in ("lq1", "lk1", "lq2", "lk2")}
    ident_in = din("ident", [128, 128])
    invf_in = din("invf", [128, 1])
    sgn_in = din("sgn", [128, 1])
    masks_in = din("masks", [NM, 128, 512])
    out = nc.dram_tensor("out", [OWN, D], F32, kind="ExternalOutput")

    QT = dscr("QT", [c.NQH, 128, OWN], BF16)
    KT = dscr("KT", [c.NQH, 128, S], BF16)
    VA = dscr("VA", [3 * c.HA, S, 128], BF16)
    VB = dscr("VB", [c.HB, S, 256], BF16)
    HTO = dscr("HTO", [128, KC * OWN], BF16)
    OAT = dscr("OAT", [128, (c.AW // 128) * OWN], BF16)
    OBT = dscr("OBT", [128, (c.BW // 128) * OWN], BF16)
    X1 = dscr("X1", [OWN, D], F32)
    X2 = dscr("X2", [OWN, D], F32)

    class _Stop(Exception):
        pass

    def chk(n):
        if stop is not None and stop == n:
            raise _Stop()

    es = ExitStack()
    ARENA_BYTES = 206 * 1024
    arena = es.enter_context(nc.sbuf_tensor("arena", [128, ARENA_BYTES // 2], BF16))
    psum = es.enter_context(nc.psum_tensor("psum", [128, 4096], F32))
    Sd = Sched(nc, es)
    K.S = Sd

    class Arena:
        def __init__(s):
            s.top = 0

        def alloc(s, nbytes):
            off = (s.top + 63) // 64 * 64
            s.top = off + nbytes
            assert s.top <= ARENA_BYTES, ("SBUF arena overflow", s.top)
            return off

    A = Arena()

    def view(off, dt, shape):
        n = 1
        for d_ in shape:
            n *= d_
        esz = 2 if dt == BF16 else 4
        a = arena[:, off // 2:(off + n * esz) // 2]
        if dt != BF16:
            a = a.bitcast(dt)
        if len(shape) == 2:
            a = a.rearrange("p (a b) -> p a b", a=shape[0])
        elif len(shape) == 3:
            a = a.rearrange("p (a b c) -> p a b c", a=shape[0], b=shape[1])
        return a

    def alloc(dt, shape):
        n = 1
        for d_ in shape:
            n *= d_
        esz = 2 if dt == BF16 else 4
        return view(A.alloc(n * esz), dt, shape)

    def bank(b, dt=F32):
        a = psum[:, b * 512:(b + 1) * 512]
        if dt == BF16:
            a = a.bitcast(BF16)
        return a

    PB = [T() for _ in range(8)]

    def mm(out, lhsT, rhs, start, stop, rd, wr, inc):
        Sd.op("pe", lambda e: e.matmul(out, lhsT, rhs, start=start, stop=stop), rd, wr, inc)

    def tr(out, in_, rd, wr, inc):
        Sd.op("pe", lambda e: e.transpose(out, in_, ident), rd, wr, inc)

    def act(out, in_, func, rd, wr, bias=0.0, scale=1.0, accum=None):
        if accum is None:
            Sd.op("act", lambda e: e.activation(out, in_, func, bias=bias, scale=scale), rd, wr)
        else:
            Sd.op("act", lambda e: e.activation(out, in_, func, bias=bias, scale=scale, accum_out=accum), rd, wr)

    def ts(out, in0, s1, s2, op0, op1, rd, wr, eng="dve"):
        if s2 is None:
            Sd.op(eng, lambda e: e.tensor_scalar(out, in0, s1, None, op0), rd, wr)
        else:
            Sd.op(eng, lambda e: e.tensor_scalar(out, in0, s1, s2, op0, op1), rd, wr)

    def tt(out, in0, in1, op, rd, wr, eng="dve"):
        Sd.op(eng, lambda e: e.tensor_tensor(out, in0, in1, op), rd, wr)

    def stt(out, in0, scalar, in1, op0, op1, rd, wr, eng="dve"):
        Sd.op(eng, lambda e: e.scalar_tensor_tensor(out, in0, scalar, in1, op0, op1), rd, wr)

    def cp(out, in_, rd, wr, eng="dve"):
        Sd.op(eng, lambda e: e.tensor_copy(out, in_), rd, wr)

    def memset(ap, val, wr, eng="dve"):
        Sd.op(eng, lambda e: e.memset(ap, val), (), wr)

    def recip(out, in_, rd, wr):
        Sd.op("dve", lambda e: e.reciprocal(out, in_), rd, wr)

    def bcast_ap(handle, off, n):
        return bass.AP(handle, off, [[0, 128], [1, n]])

    NW = 3
    WBYTES = max(KC * 256 * 2, (c.SLAB // 128) * 512 * 2, (KC // 2) * 512 * 2)
    wslots = []
    for i in range(NW):
        off = A.alloc(WBYTES)
        wslots.append((off, T()))
    wring = Ring(wslots)

    ident = alloc(BF16, [128])
    T_const = T()
    invf = alloc(F32, [1])
    sgn = alloc(F32, [1])
    validt = alloc(F32, [S // 128])
    lamt = alloc(F32, [1])
    sub08 = alloc(F32, [256])
    gpc = {n: alloc(F32, [KC]) for n in ("g_mix_pre", "g_mem_pre", "g_mem_kv", "g_mlp_pre")}
    KTm = alloc(BF16, [c.HM, c.MEM])
    VMa = alloc(BF16, [c.MEM // 128, c.HM, 129])
    T_mem = T()
    small = alloc(F32, [64])
    T_small = T()
    G_TOP = A.top

    def phase0():
        Sd.dma("pool", ident, ident_in.ap(), (), [T_const], semT=T_const)
        Sd.dma("sp", invf, invf_in.ap(), (), [T_const], semT=T_const)
        Sd.dma("sp", sgn, sgn_in.ap(), (), [T_const], semT=T_const)
        Sd.dma("sp", validt, valid.ap(), (), [T_const], semT=T_const)
        for n in gpc:
            Sd.dma("sp", gpc[n], gpc_in[n].ap(), (), [T_const], semT=T_const)
        Sd.dma("sp", sub08, bcast_ap(subln, 0, 256), (), [T_const], semT=T_const)
        tmp = alloc(F32, [4, 128])
        Tt = T()
        for i, n in enumerate(("lq1", "lk1", "lq2", "lk2")):
            Sd.dma("sp", tmp[:, i, :], bcast_ap(lam_in[n], 0, 128), (), [Tt], semT=Tt)
        tt(tmp[:, 0, :], tmp[:, 0, :], tmp[:, 1, :], ALU.mult, [Tt], [Tt])
        tt(tmp[:, 2, :], tmp[:, 2, :], tmp[:, 3, :], ALU.mult, [Tt], [Tt])
        Sd.op("dve", lambda e: e.reduce_sum(small[:, 0:1], tmp[:, 0, :], AX.X), [Tt], [T_small])
        Sd.op("dve", lambda e: e.reduce_sum(small[:, 1:2], tmp[:, 2, :], AX.X), [Tt], [T_small])
        act(small[:, 2:4], small[:, 0:2], AF.Exp, [T_small], [T_small])
        tt(lamt, small[:, 2:3], small[:, 3:4], ALU.subtract, [T_small], [T_const])
        ts(lamt, lamt, LAM_INIT, None, ALU.add, None, [T_const], [T_const])
        ts(sub08, sub08, 1.0 - LAM_INIT, None, ALU.mult, None, [T_const], [T_const])

    def rstd_from_ss(dst, ss, n, rd, wr):
        act(dst, ss, AF.Sqrt, rd, wr, bias=EPS, scale=1.0 / n)
        recip(dst, dst, wr, wr)

    tpr = Ring([0, 1])

    def norm_T(xt, Txt, hb, Thb, gname, hT, ThT, tok_off, sc0):
        ss = small[:, sc0:sc0 + 1]
        rs = small[:, sc0 + 1:sc0 + 2]
        memset(ss, 0.0, [T_small])
        chk(19)
        act(hb, xt, AF.Square, [Txt], [Thb, T_small], accum=ss)
        chk(20)
        rstd_from_ss(rs, ss, D, [T_small], [T_small])
        ts(hb, xt, rs, None, ALU.mult, None, [Txt, T_small], [Thb], eng=POOL_ENG)
        chk(21)
        g = gpc[gname]
        for k0 in range(0, KC, 8):
            b = tpr.next()
            nk = min(8, KC - k0)
            for j in range(nk):
                kc = k0 + j
                tr(bank(b, BF16)[:, j * 128:(j + 1) * 128], hb[:, kc * 128:(kc + 1) * 128], [Thb, T_const], [PB[b]],
                   inc=(j == nk - 1))
            chk(22)
            for j in range(nk):
                kc = k0 + j
                src = bank(b, BF16)[:, j * 128:(j + 1) * 128]
                dst = hT[:, kc, tok_off:tok_off + 128]
                if kc % 2 == 0 and EVAC_MODE != 1 or EVAC_MODE == 2:
                    act(dst, src, EVAC_FUNC, [PB[b], T_const], [ThT], scale=g[:, kc:kc + 1])
                else:
                    ts(dst, src, g[:, kc:kc + 1], None, ALU.mult, None, [PB[b], T_const], [ThT])
            chk(23)

    def stream(pieces, PF=2, ring=None):
        slots = [None] * len(pieces)
        if ring is None:
            ring = wring
        else:
            PF = len(ring.items) - 1

        def issue(i):
            sl = ring.next()
            pieces[i][0](sl)
            slots[i] = sl

        for i in range(min(PF, len(pieces))):
            issue(i)
        for i in range(len(pieces)):
            if i + PF < len(pieces):
                issue(i + PF)
            pieces[i][1](slots[i])

    def wload_fm(slot, w_handle, nkc, c0, ncols, sub_off=0):
        off, Tw = slot
        dst = view(off + sub_off, BF16, [nkc, ncols])
        src = w_handle.ap()[:, c0:c0 + ncols].rearrange("(kc p) c -> p kc c", p=128)
        Sd.dma("pool", dst, src, (), [Tw], semT=Tw)
        return dst

    def wload_rows(slot, w_handle, r0, nkc, c0, ncols):
        off, Tw = slot
        dst = view(off, BF16, [nkc, ncols])
        src = w_handle.ap()[r0:r0 + nkc * 128, c0:c0 + ncols].rearrange("(kc p) c -> p kc c", p=128)
        Sd.dma("pool", dst, src, (), [Tw], semT=Tw)
        return dst

    def phase_AB():
        xs = [(alloc(F32, [D]), T()) for _ in range(2)]
        hbs = [(alloc(BF16, [D]), T()) for _ in range(2)]
        hT = alloc(BF16, [KC, 1024])
        ThT = T(multi=True)
        cos2 = alloc(F32, [1024])
        sin2 = alloc(F32, [1024])
        T_rope = T()
        posi = alloc(I32, [1024])
        posf = alloc(F32, [1024])
        T_pos = T()
        rt_off = [A.alloc(4096) for _ in range(2)]
        ropet = [(view(o_, F32, [512]), view(o_ + 2048, F32, [512]), T()) for o_ in rt_off]
        ang, T_ang = view(rt_off[0], F32, [1024]), ropet[0][2]
        wtmp, T_wtmp = view(rt_off[1], F32, [1024]), ropet[1][2]
        obfs = [(alloc(BF16, [512]), T()) for _ in range(3)]
        vbfs = [(alloc(BF16, [256]), T()) for _ in range(3)]
        r_xs, r_hb, r_rt, r_ob, r_vb = Ring(xs), Ring(hbs), Ring(ropet), Ring(obfs), Ring(vbfs)
        pbr = Ring([2, 3, 4, 5, 6, 7])

        def need_tiles(kind, g):
            if kind == "q":
                return set(range(8))
            if kind in ("kb", "vb"):
                return set(range(32))
            o = [1, 2, 8][g]
            return set(range(32 - o, 32)) | set(range(0, 8 + o))

        for tg in range(S // 1024):
            for t8 in range(8):
                xt, Txt = r_xs.next()
                hb, Thb = r_hb.next()
                tok0 = tg * 1024 + t8 * 128
                Sd.dma("sp", xt, xr.ap()[tok0:tok0 + 128, :], (), [Txt], semT=Txt)
                norm_T(xt, Txt, hb, Thb, "g_mix_pre", hT, ThT, t8 * 128, 4 + 2 * (t8 % 2))
            chk(10)
            if tg == 0:
                Sd.dma("sp", HTO.ap(), hT.rearrange("p a b -> p (a b)"), [ThT], (), semT=ThT)
            chk(11)
            Sd.dma("sp", posi, bcast_ap(pos, tg * 1024, 1024), (), [T_pos], semT=T_pos)
            cp(posf, posi, [T_pos], [T_pos])
            ts(ang, posf, invf, None, ALU.mult, None, [T_pos, T_const], [T_ang])
            ts(posf, ang, 1.0 / (2 * PI), None, ALU.mult, None, [T_ang], [T_pos])
            cp(posi, posf, [T_pos], [T_pos])
            cp(posf, posi, [T_pos], [T_pos])
            stt(ang, posf, -2 * PI, ang, ALU.mult, ALU.add, [T_pos, T_ang], [T_ang])
            for (dst, shift, use_sgn) in ((sin2, 0.0, True), (cos2, 0.5 * PI, False)):
                ts(dst, ang, shift, None, ALU.add, None, [T_ang], [T_rope])
                ts(posf, dst, -PI, 2 * PI, ALU.is_lt, ALU.mult, [T_rope], [T_pos])
                ts(wtmp, dst, PI, -2 * PI, ALU.is_gt, ALU.mult, [T_rope], [T_wtmp])
                tt(dst, dst, posf, ALU.add, [T_rope, T_pos], [T_rope])
                tt(dst, dst, wtmp, ALU.add, [T_rope, T_wtmp], [T_rope])
                ts(dst, dst, -PI_SAFE, PI_SAFE, ALU.max, ALU.min, [T_rope], [T_rope])
                act(dst, dst, AF.Sin, [T_rope], [T_rope])
                if use_sgn:
                    ts(dst, dst, sgn, None, ALU.mult, None, [T_rope, T_const], [T_rope])

            chk(12)
            pieces = []

            def add_qk(kind, g, col0, head0):
                tl = need_tiles(kind, g)
                tbs = []
                for tb in (0, 1):
                    need = [t for t in range(4) if (tg * 8 + tb * 4 + t) in tl]
                    if need:
                        assert need == list(range(need[0], need[-1] + 1))
                        tbs.append((tb * 512 + need[0] * 128, len(need) * 128))
                if not tbs:
                    return
                dstT = QT if kind == "q" else KT

                def load(sl, col0=col0):
                    wload_fm(sl, w_in, KC, col0, 256)

                def comp(sl, tbs=tbs, head0=head0, dstT=dstT, kind=kind):
                    off, Tw = sl
                    w = view(off, BF16, [KC, 256])
                    for hh in range(2):
                        for (o0, n_) in tbs:
                            b = pbr.next()
                            ps = bank(b)[:, 0:n_]
                            for kc in range(KC):
                                mm(ps, w[:, kc, hh * 128:(hh + 1) * 128], hT[:, kc, o0:o0 + n_],
                                   kc == 0, kc == KC - 1, [Tw, ThT], [PB[b]], kc == KC - 1)
                            tA, tB, Trt = r_rt.next()
                            ob, Tob = r_ob.next()
                            tA, tB, ob = tA[:, 0:n_], tB[:, 0:n_], ob[:, 0:n_]
                            cs = cos2[:, o0:o0 + n_]
                            sn = sin2[:, o0:o0 + n_]
                            tt(tA, ps, cs, ALU.mult, [PB[b], T_rope], [Trt])
                            tt(tB[0:64, :], ps[64:128, :], sn[64:128, :], ALU.mult, [PB[b], T_rope], [Trt])
                            tt(tB[64:128, :], ps[0:64, :], sn[0:64, :], ALU.mult, [PB[b], T_rope], [Trt])
                            tt(ob, tA, tB, ALU.add, [Trt], [Tob])
                            if kind == "q":
                                dst = dstT.ap()[head0 + hh][:, o0:o0 + n_]
                            else:
                                t0 = tg * 1024 + o0
                                dst = dstT.ap()[head0 + hh][:, t0:t0 + n_]
                            Sd.dma("sp", dst, ob, [Tob], (), semT=Tob)

                pieces.append((load, comp))

            def add_v(kind, g, col0, head0):
                tl = need_tiles(kind, g)
                t8s = [t8 for t8 in range(8) if (tg * 8 + t8) in tl]
                if not t8s:
                    return

                def load(sl, col0=col0):
                    wload_fm(sl, w_in, KC, col0, 256)

                def comp(sl, t8s=t8s, head0=head0, kind=kind):
                    off, Tw = sl
                    w = view(off, BF16, [KC, 256])
                    for t8 in t8s:
                        b = pbr.next()
                        for kc in range(KC):
                            mm(bank(b)[:, 0:256], hT[:, kc, t8 * 128:(t8 + 1) * 128], w[:, kc, :],
                               kc == 0, kc == KC - 1, [Tw, ThT], [PB[b]], kc == KC - 1)
                        vb, Tvb = r_vb.next()
                        tok0 = tg * 1024 + t8 * 128
                        ch = tok0 // 128
                        if kind == "va":
                            ts(vb, bank(b)[:, 0:256], validt[:, ch:ch + 1], None, ALU.mult, None, [PB[b], T_const], [Tvb])
                            Sd.dma("sp", VA.ap()[head0][tok0:tok0 + 128, :], vb[:, 0:128], [Tvb], (), semT=Tvb)
                            Sd.dma("sp", VA.ap()[head0 + 1][tok0:tok0 + 128, :], vb[:, 128:256], [Tvb], (), semT=Tvb)
                        else:
                            act(vb, bank(b)[:, 0:256], AF.Copy, [PB[b]], [Tvb])
                            Sd.dma("sp", VB.ap()[head0][tok0:tok0 + 128, :], vb, [Tvb], (), semT=Tvb)

                pieces.append((load, comp))

            for g in range(3):
                for hp in range(c.HA // 2):
                    h0 = g * c.HA + hp * 2
                    add_qk("q", g, c.o_qa + h0 * 128, h0)
                    add_qk("ka", g, c.o_ka + h0 * 128, h0)
                    add_v("va", g, c.o_va + h0 * 128, h0)
            for h in range(c.HB):
                if stop == 13:
                    break
                add_qk("q", 9, c.o_qb + h * 256, 3 * c.HA + 2 * h)
                add_qk("kb", 9, c.o_kb + h * 256, 3 * c.HA + 2 * h)
                add_v("vb", 9, c.o_vb + h * 256, h)
            stream(pieces)
            chk(13)
            chk(14)

    def phase_C():
        scale = 128.0 ** -0.5
        mark0 = A.top
        p_ring = Ring([(alloc(BF16, [512]), T()) for _ in range(8)])
        st_ring = Ring([4, 5, 6])
        tp_ring = Ring([7])
        SKEW = 2
        accs = [0, 1, 2, 3]
        ostage = Ring([(alloc(BF16, [256]), T()) for _ in range(10)])
        oT_stage = Ring([(alloc(BF16, [2, 128]), T()) for _ in range(4)])
        accsb = Ring([(alloc(F32, [260]), T()) for _ in range(8)])
        pending = []

        def flush():
            while pending:
                pending.pop(0)()

        def evac_accs(E):
            res = []
            for j in range(4):
                sb, Tsb = accsb.next()
                if j % 2 == 0:
                    act(sb[:, 0:E + 1], bank(accs[j])[:, 0:E + 1], AF.Copy, [PB[accs[j]]], [Tsb])
                else:
                    cp(sb[:, 0:E + 1], bank(accs[j])[:, 0:E + 1], [PB[accs[j]]], [Tsb])
                res.append((sb, Tsb))
            return res

        mark = A.top

        masks = alloc(BF16, [NM, 512])
        Sd.dma("pool", masks, masks_in.ap().rearrange("m p f -> p m f"), (), [T_const], semT=T_const)
        offs = [128, 256, 1024]
        spans = [1024 + 2 * o for o in offs]
        sets = []
        for _ in range(2):
            d = dict(
                q=alloc(BF16, [3, OWN]), Tq=T(),
                k=[alloc(BF16, [spans[g]]) for g in range(3)], Tk=T(),
                v=[alloc(BF16, [spans[g] // 128, 129]) for g in range(3)], Tv=T(),
            )
            sets.append(d)

        def load_dil(h, d):
            for g in range(3):
                hq = g * c.HA + h
                o = offs[g]
                Sd.dma("sp", d["q"][:, g, :], QT.ap()[hq], (), [d["Tq"]], semT=d["Tq"])
                Sd.dma("sp", d["k"][g][:, 0:o], KT.ap()[hq][:, S - o:S], (), [d["Tk"]], semT=d["Tk"])
                Sd.dma("sp", d["k"][g][:, o:o + 1024 + o], KT.ap()[hq][:, 0:1024 + o], (), [d["Tk"]], semT=d["Tk"])
                nw = o // 128
                Sd.dma("sp", d["v"][g][:, 0:nw, 0:128],
                       VA.ap()[hq][S - o:S, :].rearrange("(c p) e -> p c e", p=128), (), [d["Tv"]], semT=d["Tv"])
                Sd.dma("sp", d["v"][g][:, nw:nw + 8 + nw, 0:128],
                       VA.ap()[hq][0:1024 + o, :].rearrange("(c p) e -> p c e", p=128), (), [d["Tv"]], semT=d["Tv"])
                cp(d["v"][g][:, 0:nw, 128], validt[:, S // 128 - nw:S // 128], [T_const], [d["Tv"]])
                cp(d["v"][g][:, nw:nw + 8 + nw, 128], validt[:, 0:8 + nw], [T_const], [d["Tv"]])

        def finish_dil(h, qb):
            staged = evac_accs(128)
            obs = []
            for j in range(4):
                sb, Tsb = staged[j]
                rc = small[:, 16 + j:17 + j]
                recip(rc, sb[:, 128:129], [Tsb], [T_small])
                ob, Tob = ostage.next()
                ts(ob[:, 0:128], sb[:, 0:128], rc, None, ALU.mult, None, [Tsb, T_small], [Tob])
                obs.append((ob, Tob))

            def part2():
                for j in range(4):
                    ob, Tob = obs[j]
                    b = tp_ring.next()
                    tr(bank(b, BF16)[:, 0:128], ob[:, 0:128], [Tob, T_const], [PB[b]], True)
                    oT, ToT = oT_stage.next()
                    cp(oT[:, 0, :], bank(b, BF16)[:, 0:128], [PB[b]], [ToT])
                    t0 = qb * 512 + j * 128
                    Sd.dma("sp", OAT.ap()[:, h * OWN + t0:h * OWN + t0 + 128], oT[:, 0, :], [ToT], (), semT=ToT)

            pending.append(part2)

        load_dil(0, sets[0])
        for h in range(c.HA):
            d = sets[h % 2]
            if h + 1 < c.HA:
                load_dil(h + 1, sets[(h + 1) % 2])
            for qb in range(2):
                for g in range(3):
                    gch = []
                    for (delta, mi) in plan[g]:
                        li = qb * 512 + delta + offs[g]
                        gch.append((d["k"][g][:, li:li + 128], d["Tk"], d["v"][g][:, li // 128, :], d["Tv"],
                                    masks[:, mi, :]))
                    attend_grp(d["q"][:, g, qb * 512:(qb + 1) * 512], d["Tq"], gch, 128, accs, st_ring, p_ring,
                               scale, g == 0, g == 2, skew=SKEW, hook=(flush if g == 0 else None))
                finish_dil(h, qb)
        flush()
        Sd.barrier(new_epoch=True)
        A.top = mark
        chk(2)

        NCH = S // 128
        sets = []
        for _ in range(2):
            d = dict(q=alloc(BF16, [2, OWN]), Tq=T(), k=alloc(BF16, [2, S]), Tk=T(),
                     v=alloc(BF16, [NCH, 257]), Tv=T())
            sets.append(d)
        for d in sets:
            memset(d["v"][:, :, 256:257], 1.0, [d["Tv"]])
        on1 = alloc(F32, [4, 256])
        T_on1 = [T() for _ in range(4)]
        ods = Ring([(alloc(F32, [256]), T()) for _ in range(2)])

        def load_diff(h, d):
            for cc in range(2):
                hq = 3 * c.HA + 2 * h + cc
                Sd.dma("sp", d["q"][:, cc, :], QT.ap()[hq], (), [d["Tq"]], semT=d["Tq"])
                Sd.dma("sp", d["k"][:, cc, :], KT.ap()[hq], (), [d["Tk"]], semT=d["Tk"])
            Sd.dma("sp", d["v"][:, :, 0:256], VB.ap()[h].rearrange("(c p) e -> p c e", p=128), (), [d["Tv"]],
                   semT=d["Tv"])

        def finish_diff(h, qb, cc):
            staged = evac_accs(256)
            obs = []
            for j in range(4):
                sb, Tsb = staged[j]
                rc = small[:, 16 + j:17 + j]
                recip(rc, sb[:, 256:257], [Tsb], [T_small])
                if cc == 0:
                    ts(on1[:, j, :], sb[:, 0:256], rc, None, ALU.mult, None, [Tsb, T_small], [T_on1[j]])
                    continue
                od, T_od = ods.next()
                tt(rc, rc, lamt, ALU.mult, [T_small, T_const], [T_small])
                ts(od, sb[:, 0:256], rc, None, ALU.mult, None, [Tsb, T_small], [T_od])
                tt(od, on1[:, j, :], od, ALU.subtract, [T_on1[j], T_od], [T_od])
                ss = small[:, 24 + j:25 + j]
                rs = small[:, 28 + j:29 + j]
                ob, Tob = ostage.next()
                memset(ss, 0.0, [T_small])
                act(ob, od, AF.Square, [T_od], [Tob, T_small], accum=ss)
                rstd_from_ss(rs, ss, 256, [T_small], [T_small])
                stt(ob, od, rs, sub08, ALU.mult, ALU.mult, [T_od, T_small, T_const], [Tob])
                obs.append((ob, Tob))
            if cc == 0:
                return

            def part2():
                for j in range(4):
                    ob, Tob = obs[j]
                    oT, ToT = oT_stage.next()
                    for e2 in range(2):
                        b = tp_ring.next()
                        tr(bank(b, BF16)[:, 0:128], ob[:, e2 * 128:(e2 + 1) * 128], [Tob, T_const], [PB[b]], True)
                        cp(oT[:, e2, :], bank(b, BF16)[:, 0:128], [PB[b]], [ToT])
                    t0 = qb * 512 + j * 128
                    for e2 in range(2):
                        fc = 2 * h + e2
                        Sd.dma("sp", OBT.ap()[:, fc * OWN + t0:fc * OWN + t0 + 128], oT[:, e2, :], [ToT], (),
                               semT=ToT)

            pending.append(part2)

        load_diff(0, sets[0])
        for h in range(c.HB):
            d = sets[h % 2]
            if h + 1 < c.HB:
                load_diff(h + 1, sets[(h + 1) % 2])
            for qb in range(2):
                for cc in range(2):
                    chunks = [(d["k"][:, cc, ci * 128:(ci + 1) * 128], d["Tk"], d["v"][:, ci, :], d["Tv"], None)
                              for ci in range(NCH)]
                    attend_grp(d["q"][:, cc, qb * 512:(qb + 1) * 512], d["Tq"], chunks, 256, accs, st_ring, p_ring,
                               scale, True, True, skew=SKEW, hook=flush)
                    finish_diff(h, qb, cc)
        flush()
        Sd.barrier()
        A.top = mark0

    def attend_grp(qT, Tq, chunks, E, accs, st_ring, p_ring, scale, first, last, skew=1, hook=None):
        n = len(chunks)
        stb = [None] * n

        def qk(ci):
            b = st_ring.next()
            stb[ci] = b
            mm(bank(b), chunks[ci][0], qT, True, True, [chunks[ci][1], Tq], [PB[b]], True)

        for ci in range(min(skew, n)):
            qk(ci)
        for ci in range(n):
            if ci + skew < n:
                qk(ci + skew)
            kt, Tk, va, Tv, mk = chunks[ci]
            b = stb[ci]
            pt, Tp = p_ring.next()
            act(pt, bank(b), AF.Exp, [PB[b]], [Tp], scale=scale)
            if mk is not None:
                tt(pt, pt, mk, ALU.mult, [Tp, T_const], [Tp])
            for j in range(4):
                mm(bank(accs[j])[:, 0:E + 1], pt[:, j * 128:(j + 1) * 128], va, first and ci == 0,
                   last and ci == n - 1, [Tp, Tv], [PB[accs[j]]], j == 3)
            if hook is not None and ci == min(5, n - 1):
                hook()

    def phase_M():
        mark = A.top
        xs = [(alloc(F32, [D]), T()) for _ in range(2)]
        hbs = [(alloc(BF16, [D]), T()) for _ in range(2)]
        mT = alloc(BF16, [KC, c.MEM])
        TmT = T(multi=True)
        for t in range(c.MEM // 128):
            xt, Txt = xs[t % 2]
            hb, Thb = hbs[t % 2]
            Sd.dma("sp", xt, memx.ap()[t * 128:(t + 1) * 128, :], (), [Txt], semT=Txt)
            norm_T(xt, Txt, hb, Thb, "g_mem_kv", mT, TmT, t * 128, 4 + 2 * t)
        memset(VMa[:, :, :, 128:129], 1.0, [T_mem])
        pbr = Ring([2, 3, 4, 5])
        pieces = []
        for hm in range(c.HM):
            def loadk(sl, hm=hm):
                wload_fm(sl, w_mkv, KC, hm * 128, 128)

            def compk(sl, hm=hm):
                off, Tw = sl
                w = view(off, BF16, [KC, 128])
                b = pbr.next()
                for kc in range(KC):
                    mm(bank(b)[:, 0:c.MEM], w[:, kc, :], mT[:, kc, :], kc == 0, kc == KC - 1, [Tw, TmT], [PB[b]],
                       kc == KC - 1)
                act(KTm[:, hm, :], bank(b)[:, 0:c.MEM], AF.Copy, [PB[b]], [T_mem])

            def loadv(sl, hm=hm):
                wload_fm(sl, w_mkv, KC, c.MW + hm * 128, 128)

            def compv(sl, hm=hm):
                off, Tw = sl
                w = view(off, BF16, [KC, 128])
                for t in range(c.MEM // 128):
                    b = pbr.next()
                    for kc in range(KC):
                        mm(bank(b)[:, 0:128], mT[:, kc, t * 128:(t + 1) * 128], w[:, kc, :], kc == 0, kc == KC - 1,
                           [Tw, TmT], [PB[b]], kc == KC - 1)
                    act(VMa[:, t, hm, 0:128], bank(b)[:, 0:128], AF.Copy, [PB[b]], [T_mem])

            pieces.append((loadk, compk))
            pieces.append((loadv, compv))
        stream(pieces)
        Sd.barrier()
        A.top = mark

    def norm_residual(ybuf, Ty, src_dram, dst_dram, gpost_name, gpre_name, hTn, ThTn, tokbase):
        gpost = alloc(F32, [D])
        xts = [(alloc(F32, [D]), T()) for _ in range(1)]
        hbs2 = [(alloc(BF16, [D]), T()) for _ in range(1)]
        Tg = T()
        Sd.dma("sp", gpost, bcast_ap(gv[gpost_name], 0, D), (), [Tg], semT=Tg)
        for t4 in range(4):
            xt, Txt = xts[t4 % len(xts)]
            hb, Thb = hbs2[t4 % len(hbs2)]
            tok0 = tokbase + t4 * 128
            Sd.dma("sp", xt, src_dram.ap()[tok0:tok0 + 128, :], (), [Txt], semT=Txt)
            ss = small[:, 32 + 2 * (t4 % 2):33 + 2 * (t4 % 2)]
            rs = small[:, 33 + 2 * (t4 % 2):34 + 2 * (t4 % 2)]
            y = ybuf[:, t4, :]
            memset(ss, 0.0, [T_small])
            act(hb, y, AF.Square, [Ty], [Thb, T_small], accum=ss)
            rstd_from_ss(rs, ss, D, [T_small], [T_small])
            stt(y, y, rs, gpost, ALU.mult, ALU.mult, [Ty, T_small, Tg], [Ty])
            tt(xt, xt, y, ALU.add, [Txt, Ty], [Txt], eng=POOL_ENG)
            Sd.dma("sp", dst_dram.ap()[tok0:tok0 + 128, :], xt, [Txt], (), semT=Txt)
            if gpre_name is not None:
                norm_T(xt, Txt, hb, Thb, gpre_name, hTn, ThTn, t4 * 128, 36 + 2 * (t4 % 2))

    MLP_T = [T() for _ in range(MLP_EXTRA)]

    def post_half(hf):
        tokbase = hf * 512
        mark = A.top
        scale = 128.0 ** -0.5
        NA, NB = c.AW // 128, c.BW // 128
        hTreg = alloc(BF16, [KC, 512])
        yoff = A.alloc(4 * D * 4)
        ybuf = view(yoff, F32, [4, D])
        Ty = T(multi=True)
        Xbase = A.top

        hTo = hTreg
        T_in = T()
        mT = alloc(BF16, [KC, 512])
        TmT = T(multi=True)
        oaT = alloc(BF16, [NA, 512])
        o2 = NB * 512 * 2
        soff = yoff if o2 + 4 * 2048 <= 4 * D * 4 else A.alloc(o2 + 4 * 2048)
        obT = view(soff, BF16, [NB, 512])
        sg = [[view(soff + o2 + (k * 2 + j) * 2048, F32, [512]) for j in range(2)] for k in range(2)]
        Tsg = [[T() for j in range(2)] for k in range(2)]
        Sd.dma("sp", hTo, HTO.ap().rearrange("p (a b) -> p a b", a=KC)[:, :, tokbase:tokbase + 512], (), [T_in],
               semT=T_in)
        Sd.dma("sp", oaT, OAT.ap().rearrange("p (a b) -> p a b", a=NA)[:, :, tokbase:tokbase + 512], (), [T_in],
               semT=T_in)
        Sd.dma("sp", obT, OBT.ap().rearrange("p (a b) -> p a b", a=NB)[:, :, tokbase:tokbase + 512], (), [T_in],
               semT=T_in)
        pieces = []
        for cc in range(D // 256):
            def load_g(sl, cc=cc, k=0):
                wload_fm(sl, w_in, KC, (c.o_ga if k == 0 else c.o_gb) + cc * 256, 256)

            def comp_g(sl, cc=cc, k=0):
                off, Tw = sl
                w = view(off, BF16, [KC, 256])
                for j in range(2):
                    b = 2 * k + j
                    for kc in range(KC):
                        mm(bank(b), w[:, kc, j * 128:(j + 1) * 128], hTo[:, kc, :], kc == 0, kc == KC - 1,
                           [Tw, T_in], [PB[b]], kc == KC - 1)
                    act(sg[k][j], bank(b), AF.Sigmoid, [PB[b]], [Tsg[k][j]])

            def load_ab(sl, cc=cc):
                wload_fm(sl, w_a, NA, cc * 256, 256)
                wload_fm(sl, w_b, NB, cc * 256, 256, sub_off=NA * 256 * 2)

            def comp_ab(sl, cc=cc):
                off_a, Tab = sl
                wa = view(off_a, BF16, [NA, 256])
                wb = view(off_a + NA * 256 * 2, BF16, [NB, 256])
                for j in range(2):
                    cs = slice(j * 128, (j + 1) * 128)
                    b2, b3 = 4 + j, 6 + j
                    for kc in range(NA):
                        mm(bank(b2), wa[:, kc, cs], oaT[:, kc, :], kc == 0, kc == NA - 1, [Tab, T_in], [PB[b2]],
                           kc == NA - 1)
                    for kc in range(NB):
                        mm(bank(b3), wb[:, kc, cs], obT[:, kc, :], kc == 0, kc == NB - 1, [Tab, T_in], [PB[b3]],
                           kc == NB - 1)
                    tt(sg[0][j], sg[0][j], bank(b2), ALU.mult, [Tsg[0][j], PB[b2]], [Tsg[0][j]])
                    tt(sg[1][j], sg[1][j], bank(b3), ALU.mult, [Tsg[1][j], PB[b3]], [Tsg[1][j]])
                    tt(mT[:, cc * 2 + j, :], sg[0][j], sg[1][j], ALU.add, [Tsg[0][j], Tsg[1][j]], [TmT])

            pieces.append((load_g, comp_g))
            pieces.append((lambda sl, cc=cc: load_g(sl, cc, 1), lambda sl, cc=cc: comp_g(sl, cc, 1)))
            pieces.append((load_ab, comp_ab))
        stream(pieces)
        Sd.barrier()

        def proj_tm(aT, TaT, nkc_total, w_handle, row0):
            nsp = 2 if nkc_total >= 2 and nkc_total * 512 * 2 > WBYTES else 1
            nk = nkc_total // nsp
            pcs = []
            pbs = Ring([(0, 1, 2, 3), (4, 5, 6, 7)])
            state = {}
            for ct in range(D // 512):
                for kh in range(nsp):
                    def load(sl, ct=ct, kh=kh):
                        wload_rows(sl, w_handle, row0 + kh * nk * 128, nk, ct * 512, 512)

                    def comp(sl, ct=ct, kh=kh):
                        off, Tw = sl
                        w = view(off, BF16, [nk, 512])
                        if kh == 0:
                            state["b"] = pbs.next()
                        bs = state["b"]
                        for t4 in range(4):
                            for kl in range(nk):
                                kc = kh * nk + kl
                                mm(bank(bs[t4]), aT[:, kc, t4 * 128:(t4 + 1) * 128], w[:, kl, :],
                                   kc == 0, kc == nkc_total - 1, [TaT, Tw], [PB[bs[t4]]], kl == nk - 1)
                        if kh == nsp - 1:
                            for t4 in range(4):
                                dst = ybuf[:, t4, ct * 512:(ct + 1) * 512]
                                if t4 % 2 == 0:
                                    act(dst, bank(bs[t4]), AF.Copy, [PB[bs[t4]]], [Ty])
                                else:
                                    cp(dst, bank(bs[t4]), [PB[bs[t4]]], [Ty])

                    pcs.append((load, comp))
            stream(pcs)

        proj_tm(mT, TmT, KC, w_mix, 0)
        Sd.barrier()
        A.top = Xbase
        h2T = hTreg
        Th2 = T(multi=True)
        norm_residual(ybuf, Ty, xr, X1, "g_mix_post", "g_mem_pre", h2T, Th2, tokbase)
        Sd.barrier()
        A.top = Xbase
        QTm = alloc(BF16, [c.HM, 512])
        TQm = T()
        oTm = alloc(BF16, [c.HM, 512])
        ToTm = T(multi=True)
        p_ring = Ring([(alloc(BF16, [512]), T()) for _ in range(4)])
        ost = Ring([(alloc(BF16, [128]), T()) for _ in range(2)])
        pbr = Ring([6, 7])
        pieces = []
        for hm in range(c.HM):
            def load(sl, hm=hm):
                wload_fm(sl, w_mq, KC, hm * 128, 128)

            def comp(sl, hm=hm):
                off, Tw = sl
                w = view(off, BF16, [KC, 128])
                b = pbr.next()
                for kc in range(KC):
                    mm(bank(b), w[:, kc, :], h2T[:, kc, :], kc == 0, kc == KC - 1, [Tw, Th2], [PB[b]], kc == KC - 1)
                act(QTm[:, hm, :], bank(b), AF.Copy, [PB[b]], [TQm])

            pieces.append((load, comp))
        stream(pieces)
        st_ring = Ring([4, 5])
        accs = [0, 1, 2, 3]
        tp_ring = Ring([6, 7])
        for hm in range(c.HM):
            chunks = [(KTm[:, hm, ci * 128:(ci + 1) * 128], T_mem, VMa[:, ci, hm, :], T_mem, None)
                      for ci in range(c.MEM // 128)]
            attend_grp(QTm[:, hm, :], TQm, chunks, 128, accs, st_ring, p_ring, scale, True, True)
            for j in range(4):
                a = bank(accs[j])
                rc = small[:, 16 + j:17 + j]
                recip(rc, a[:, 128:129], [PB[accs[j]]], [T_small])
                ob, Tob = ost.next()
                ts(ob, a[:, 0:128], rc, None, ALU.mult, None, [PB[accs[j]], T_small], [Tob])
                b = tp_ring.next()
                tr(bank(b, BF16)[:, 0:128], ob, [Tob, T_const], [PB[b]], True)
                cp(oTm[:, hm, j * 128:(j + 1) * 128], bank(b, BF16)[:, 0:128], [PB[b]], [ToTm])
        Sd.barrier()
        proj_tm(oTm, ToTm, c.HM, w_mo, 0)
        Sd.barrier()
        A.top = Xbase
        h3T = hTreg
        Th3 = T(multi=True)
        norm_residual(ybuf, Ty, X1, X2, "g_mem_post", "g_mlp_pre", h3T, Th3, tokbase)
        Sd.barrier()
        A.top = Xbase
        NFC = c.SLAB // 128
        uT = alloc(BF16, [NFC, 512])
        TuT = T(multi=True)
        rtmp = Ring([(alloc(F32, [512]), T()) for _ in range(2)])
        pbu = Ring([4, 5, 6, 7])
        mlp_ring = Ring(list(wslots) + [(A.alloc(WBYTES), MLP_T[i]) for i in range(MLP_EXTRA)])
        for sl_i in range(c.FF // c.SLAB):
            ff0 = sl_i * c.SLAB
            pieces = []
            for pp in range(c.SLAB // 256):
                def load(sl, pp=pp, ff0=ff0):
                    wload_fm(sl, w_up, KC, ff0 + pp * 256, 256)

                def comp(sl, pp=pp):
                    off, Tw = sl
                    w = view(off, BF16, [KC, 256])
                    for j in range(2):
                        b = pbu.next()
                        for kc in range(KC):
                            mm(bank(b), w[:, kc, j * 128:(j + 1) * 128], h3T[:, kc, :], kc == 0, kc == KC - 1,
                               [Tw, Th3], [PB[b]], kc == KC - 1)
                        r, Tr = rtmp.next()
                        act(r, bank(b), AF.Relu, [PB[b]], [Tr])
                        tt(uT[:, pp * 2 + j, :], r, r, ALU.mult, [Tr], [TuT])

                pieces.append((load, comp))
            for ct in range(D // 512):
                def load(sl, ct=ct, ff0=ff0):
                    wload_rows(sl, w_dn, ff0, NFC, ct * 512, 512)

                def comp(sl, ct=ct, first=(sl_i == 0)):
                    off, Tw = sl
                    w = view(off, BF16, [NFC, 512])
                    for t4 in range(4):
                        b = t4
                        for fc in range(NFC):
                            mm(bank(b), uT[:, fc, t4 * 128:(t4 + 1) * 128], w[:, fc, :], fc == 0, fc == NFC - 1,
                               [TuT, Tw], [PB[b]], fc == NFC - 1)
                        dst = ybuf[:, t4, ct * 512:(ct + 1) * 512]
                        if first:
                            cp(dst, bank(b), [PB[b]], [Ty])
                        else:
                            tt(dst, dst, bank(b), ALU.add, [Ty, PB[b]], [Ty])

                pieces.append((load, comp))
            stream(pieces, ring=mlp_ring)
        Sd.barrier()
        A.top = Xbase
        norm_residual(ybuf, Ty, X2, out, "g_mlp_post", None, None, None, tokbase)
        Sd.barrier(new_epoch=True)
        A.top = mark

    try:
        phase0()
        chk(0)
        markAB = A.top
        phase_AB()
        Sd.barrier(new_epoch=True)
        chk(1)
        A.top = markAB
        phase_C()
        chk(3)
        phase_M()
        chk(4)
        for hf in range(2):
            post_half(hf)
    except _Stop:
        pass
    Sd.barrier()

    with nc.Block() as block:
        Sd.emit(block)
    es.close()
    return nc


def make_in_maps(cfg, x, mem, positions, norm_mix_pre, w_in, w_a, w_b, w_mix_out, norm_mix_post,
                 lambda_q1, lambda_k1, lambda_q2, lambda_k2, diff_subln,
                 norm_mem_pre, norm_mem_kv, w_mem_q, w_mem_kv, w_mem_o, norm_mem_post,
                 norm_mlp_pre, w_mlp_up, w_mlp_down, norm_mlp_post):
    c = cfg
    S = c.S
    f = lambda a: np.ascontiguousarray(np.asarray(a, dtype=np.float32))
    inv = (10000.0 ** (-np.arange(0, 128, 2, dtype=np.float32) / np.float32(128))).astype(np.float32)
    invf = np.concatenate([inv, inv]).reshape(128, 1).astype(np.float32)
    sgn = np.concatenate([np.ones(64), -np.ones(64)]).reshape(128, 1).astype(np.float32)
    shared = dict(
        w_in=f(w_in[0]), w_a=f(w_a[0]), w_b=f(w_b[0]), w_mix=f(w_mix_out[0]), w_mq=f(w_mem_q[0]),
        w_mkv=f(w_mem_kv[0]), w_mo=f(w_mem_o[0]), w_up=f(w_mlp_up[0]), w_dn=f(w_mlp_down[0]),
        g_mix_pre=f(norm_mix_pre), g_mix_post=f(norm_mix_post), g_mem_pre=f(norm_mem_pre),
        g_mem_kv=f(norm_mem_kv), g_mem_post=f(norm_mem_post), g_mlp_pre=f(norm_mlp_pre),
        g_mlp_post=f(norm_mlp_post), subln=f(diff_subln), lq1=f(lambda_q1), lk1=f(lambda_k1),
        lq2=f(lambda_q2), lk2=f(lambda_k2), ident=np.eye(128, dtype=np.float32), invf=invf, sgn=sgn,
        masks=build_masks(),
    )
    for n in ("g_mix_pre", "g_mem_pre", "g_mem_kv", "g_mlp_pre"):
        shared["pc_" + n] = np.ascontiguousarray(shared[n].reshape(c.KC, 128).T)
    x = np.asarray(x, dtype=np.float32)
    mem = np.asarray(mem, dtype=np.float32)
    positions = np.asarray(positions, dtype=np.int32)
    maps = []
    for core in range(NCORES):
        b, r = core // 4, core % 4
        sh = OWN * r
        xr = np.ascontiguousarray(np.roll(x[b], -sh, axis=0))
        pr = np.ascontiguousarray(np.roll(positions[b], -sh)).reshape(1, S)
        j = np.arange(S)
        real = np.where(j < 2048, sh + j, sh + j - S)
        v = ((real >= 0) & (real < S) & ((j < 2048) | (j >= 3072))).astype(np.float32)
        valid = np.ascontiguousarray(v.reshape(S // 128, 128).T)
        m = dict(shared)
        m.update(xr=xr, pos=pr, valid=valid, mem=np.ascontiguousarray(mem[b]))
        maps.append(m)
    return maps


_NC_CACHE = {}


def run(cfg, inputs, stop=None):
    key = (cfg.D, cfg.HA, cfg.HB, cfg.HM, cfg.FF, stop)
    if key not in _NC_CACHE:
        _NC_CACHE[key] = build(cfg, stop)
    nc = _NC_CACHE[key]
    maps = make_in_maps(cfg, **inputs)
    res = run_bass_kernel_spmd(nc, maps, core_ids=list(range(NCORES)))
    outp = np.zeros((2, cfg.S, cfg.D), np.float32)
    for core in range(NCORES):
        b, r = core // 4, core % 4
        outp[b, OWN * r:OWN * (r + 1), :] = res.results[core]["out"]
    return outp


def kernel(**inputs):
    return run(Cfg(), inputs)
```

```python
import math
from contextlib import ExitStack

import numpy as np
import concourse.bass as bass
import concourse.mybir as mybir
from concourse.bass_utils import run_bass_kernel_spmd

F32 = mybir.dt.float32
BF16 = mybir.dt.bfloat16
I32 = mybir.dt.int32
ALU = mybir.AluOpType
AF = mybir.ActivationFunctionType
AX = mybir.AxisListType

EPS = 1e-6
PI = math.pi
PI_SAFE = 3.141592
LAM_INIT = 0.8 - 0.6 * 1.0
DIL_CFG = ((128, 1), (512, 4), (2048, 16))
NCORES = 8
import os
EVAC_MODE = int(os.environ.get('EVAC_MODE', '1'))
POOL_ENG = os.environ.get('POOL_ENG', 'dve')
MLP_EXTRA = int(os.environ.get('MLP_EXTRA', '2'))
EVAC_FUNC = AF.Identity if os.environ.get('EVAC_ID') else AF.Copy
OWN = 1024


class Cfg:
    def __init__(s, D=4096, HA=8, HB=8, HM=4, FF=16384, S=4096, MEM=256):
        s.D, s.HA, s.HB, s.HM, s.FF, s.S, s.MEM = D, HA, HB, HM, FF, S, MEM
        s.G = 3
        s.KC = D // 128
        s.QAW = 3 * HA * 128
        s.QBW = HB * 256
        s.o_qa = 0
        s.o_ka = s.QAW
        s.o_va = 2 * s.QAW
        s.o_qb = 3 * s.QAW
        s.o_kb = s.o_qb + s.QBW
        s.o_vb = s.o_kb + s.QBW
        s.o_ga = s.o_vb + s.QBW
        s.o_gb = s.o_ga + D
        s.DIN = s.o_gb + D
        s.AW = HA * 128
        s.BW = HB * 256
        s.MW = HM * 128
        s.NQH = 3 * HA + 2 * HB
        s.SLAB = min(2048, FF)


def dil_chunk_plan():
    masks = []
    plan = []
    for g, (win, dil) in enumerate(DIL_CFG):
        half = win // 2
        lo = -((half + 127) // 128) * 128
        hi = 512 + ((half + 127) // 128) * 128
        lst = []
        interior = None
        for delta in range(lo, hi, 128):
            is_int = (delta + 127 <= half) and (delta - 511 >= -half)
            if is_int and interior is not None:
                lst.append((delta, interior))
                continue
            masks.append((g, delta))
            if is_int:
                interior = len(masks) - 1
            lst.append((delta, len(masks) - 1))
        plan.append(lst)
    return plan, masks


def build_masks():
    plan, masks = dil_chunk_plan()
    out = np.zeros((len(masks), 128, 512), np.float32)
    p = np.arange(128)[:, None]
    f = np.arange(512)[None, :]
    for i, (g, delta) in enumerate(masks):
        win, dil = DIL_CFG[g]
        d = delta + p - f
        out[i] = ((np.abs(d) <= win // 2) & (d % dil == 0)).astype(np.float32)
    return out


class T:
    __slots__ = ("w", "r", "multi", "sem", "cnt")

    def __init__(s, multi=False):
        s.w = {}
        s.r = {}
        s.multi = multi
        s.sem = None
        s.cnt = 0


ENGS = ("pe", "act", "dve", "pool", "sp")


class Sched:
    def __init__(s, nc, es):
        s.nc = nc
        s.es = es
        s.prog = {e: [] for e in ENGS}
        s.esem = {e: es.enter_context(nc.semaphore("es_" + e)) for e in ENGS}
        s.cnt = {e: 0 for e in ENGS}
        s.seen = {e: {} for e in ENGS}
        s.semname = {}
        s.nsem = 0
        s.dma_sems = []
        for e in ENGS:
            s.semname[id(s.esem[e])] = s.esem[e]

    def _need(s, eng, rd, wr):
        deps = {}

        def mg(d):
            for k, v in d.items():
                if deps.get(k, 0) < v:
                    deps[k] = v

        for t in rd:
            mg(t.w)
        for t in wr:
            mg(t.r)
            if not t.multi:
                mg(t.w)
        own = id(s.esem[eng])
        seen = s.seen[eng]
        for k, v in deps.items():
            if k == own and eng == "pe":
                continue
            if seen.get(k, 0) < v:
                seen[k] = v
                s.prog[eng].append(("w", s.semname[k], v))

    def _record(s, ev, rd, wr):
        k, v = ev
        for t in rd:
            if t.r.get(k, 0) < v:
                t.r[k] = v
        for t in wr:
            if t.w.get(k, 0) < v:
                t.w[k] = v

    def op(s, eng, fn, rd=(), wr=(), inc=True):
        s._need(eng, rd, wr)
        if inc:
            s.cnt[eng] += 1
            ev = (id(s.esem[eng]), s.cnt[eng])
        else:
            ev = (id(s.esem[eng]), s.cnt[eng] + 1)
        s.prog[eng].append(("o", fn, s.esem[eng] if inc else None))
        s._record(ev, rd, wr)

    def dma(s, q, out, in_, rd=(), wr=(), semT=None, slow=False):
        s._need(q, rd, wr)
        if semT.sem is None:
            semT.sem = {}
            semT.cnt = {}
        if q not in semT.sem:
            s.nsem += 1
            sem = s.es.enter_context(s.nc.semaphore("ds%d" % s.nsem))
            semT.sem[q] = sem
            semT.cnt[q] = 0
            s.semname[id(sem)] = sem
            s.dma_sems.append((semT, q))
        semT.cnt[q] += 16
        s.prog[q].append(("d", out, in_, semT.sem[q], slow))
        s._record((id(semT.sem[q]), semT.cnt[q]), rd, wr)

    def barrier(s, new_epoch=False):
        s._barrier()
        if new_epoch:
            for e in ENGS:
                s.esem[e] = s.es.enter_context(s.nc.semaphore("es_%s_%d" % (e, s.nsem)))
                s.nsem += 1
                s.semname[id(s.esem[e])] = s.esem[e]
                s.cnt[e] = 0

    def _barrier(s):
        for e in ENGS:
            seen = s.seen[e]
            for e2 in ENGS:
                if e2 == e:
                    continue
                k = id(s.esem[e2])
                v = s.cnt[e2]
                if v > 0 and seen.get(k, 0) < v:
                    seen[k] = v
                    s.prog[e].append(("w", s.esem[e2], v))
            for (t, q) in s.dma_sems:
                k = id(t.sem[q])
                v = t.cnt[q]
                if v > 0 and seen.get(k, 0) < v:
                    seen[k] = v
                    s.prog[e].append(("w", t.sem[q], v))

    def emit(s, block):
        def replay(eng_name):
            lst = s.prog[eng_name]

            def f(e):
                for it in lst:
                    if it[0] == "w":
                        e.wait_ge(it[1], it[2])
                    elif it[0] == "o":
                        ins = it[1](e)
                        if it[2] is not None:
                            ins.then_inc(it[2], 1)
                    elif it[4]:
                        e.dma_start(out=it[1], in_=it[2], allow_slow_non_contiguous=True).then_inc(it[3], 16)
                    else:
                        e.dma_start(out=it[1], in_=it[2]).then_inc(it[3], 16)

            return f

        block.tensor(replay("pe"))
        block.scalar(replay("act"))
        block.vector(replay("dve"))
        block.gpsimd(replay("pool"))
        block.sync(replay("sp"))


class Ring:
    def __init__(s, items):
        s.items = items
        s.i = 0

    def next(s):
        it = s.items[s.i % len(s.items)]
        s.i += 1
        return it


class KB:
    pass


def build(cfg, stop=None):
    c = cfg
    D, KC, S = c.D, c.KC, c.S
    nc = bass.Bass("TRN2", target_bir_lowering=False)
    K = KB()
    K.nc, K.c = nc, c
    plan, mask_list = dil_chunk_plan()
    NM = len(mask_list)

    def din(name, shape, dt=F32):
        return nc.dram_tensor(name, list(shape), dt, kind="ExternalInput")

    def dscr(name, shape, dt):
        return nc.dram_tensor(name, list(shape), dt, kind="Internal")

    xr = din("xr", [S, D])
    pos = din("pos", [1, S], I32)
    valid = din("valid", [128, S // 128])
    memx = din("mem", [c.MEM, D])
    w_in = din("w_in", [D, c.DIN])
    w_a = din("w_a", [c.AW, D])
    w_b = din("w_b", [c.BW, D])
    w_mix = din("w_mix", [D, D])
    w_mq = din("w_mq", [D, c.MW])
    w_mkv = din("w_mkv", [D, 2 * c.MW])
    w_mo = din("w_mo", [c.MW, D])
    w_up = din("w_up", [D, c.FF])
    w_dn = din("w_dn", [c.FF, D])
    gnames = ["g_mix_pre", "g_mix_post", "g_mem_pre", "g_mem_kv", "g_mem_post", "g_mlp_pre", "g_mlp_post"]
    gv = {n: din(n, [1, D]) for n in gnames}
    gpc_in = {n: din("pc_" + n, [128, KC]) for n in ("g_mix_pre", "g_mem_pre", "g_mem_kv", "g_mlp_pre")}
    subln = din("subln", [1, 256])
    lam_in = {n: din(n, [1, 128]) for n in ("lq1", "lk1", "lq2", "lk2")}
    ident_in = din("ident", [128, 128])
    invf_in = din("invf", [128, 1])
    sgn_in = din("sgn", [128, 1])
    masks_in = din("masks", [NM, 128, 512])
    out = nc.dram_tensor("out", [OWN, D], F32, kind="ExternalOutput")

    QT = dscr("QT", [c.NQH, 128, OWN], BF16)
    KT = dscr("KT", [c.NQH, 128, S], BF16)
    VA = dscr("VA", [3 * c.HA, S, 128], BF16)
    VB = dscr("VB", [c.HB, S, 256], BF16)
    HTO = dscr("HTO", [128, KC * OWN], BF16)
    OAT = dscr("OAT", [128, (c.AW // 128) * OWN], BF16)
    OBT = dscr("OBT", [128, (c.BW // 128) * OWN], BF16)
    X1 = dscr("X1", [OWN, D], F32)
    X2 = dscr("X2", [OWN, D], F32)

    class _Stop(Exception):
        pass

    def chk(n):
        if stop is not None and stop == n:
            raise _Stop()

    es = ExitStack()
    ARENA_BYTES = 206 * 1024
    arena = es.enter_context(nc.sbuf_tensor("arena", [128, ARENA_BYTES // 2], BF16))
    psum = es.enter_context(nc.psum_tensor("psum", [128, 4096], F32))
    Sd = Sched(nc, es)
    K.S = Sd

    class Arena:
        def __init__(s):
            s.top = 0

        def alloc(s, nbytes):
            off = (s.top + 63) // 64 * 64
            s.top = off + nbytes
            assert s.top <= ARENA_BYTES, ("SBUF arena overflow", s.top)
            return off

    A = Arena()

    def view(off, dt, shape):
        n = 1
        for d_ in shape:
            n *= d_
        esz = 2 if dt == BF16 else 4
        a = arena[:, off // 2:(off + n * esz) // 2]
        if dt != BF16:
            a = a.bitcast(dt)
        if len(shape) == 2:
            a = a.rearrange("p (a b) -> p a b", a=shape[0])
        elif len(shape) == 3:
            a = a.rearrange("p (a b c) -> p a b c", a=shape[0], b=shape[1])
        return a

    def alloc(dt, shape):
        n = 1
        for d_ in shape:
            n *= d_
        esz = 2 if dt == BF16 else 4
        return view(A.alloc(n * esz), dt, shape)

    def bank(b, dt=F32):
        a = psum[:, b * 512:(b + 1) * 512]
        if dt == BF16:
            a = a.bitcast(BF16)
        return a

    PB = [T() for _ in range(8)]

    def mm(out, lhsT, rhs, start, stop, rd, wr, inc):
        Sd.op("pe", lambda e: e.matmul(out, lhsT, rhs, start=start, stop=stop), rd, wr, inc)

    def tr(out, in_, rd, wr, inc):
        Sd.op("pe", lambda e: e.transpose(out, in_, ident), rd, wr, inc)

    def act(out, in_, func, rd, wr, bias=0.0, scale=1.0, accum=None):
        if accum is None:
            Sd.op("act", lambda e: e.activation(out, in_, func, bias=bias, scale=scale), rd, wr)
        else:
            Sd.op("act", lambda e: e.activation(out, in_, func, bias=bias, scale=scale, accum_out=accum), rd, wr)

    def ts(out, in0, s1, s2, op0, op1, rd, wr, eng="dve"):
        if s2 is None:
            Sd.op(eng, lambda e: e.tensor_scalar(out, in0, s1, None, op0), rd, wr)
        else:
            Sd.op(eng, lambda e: e.tensor_scalar(out, in0, s1, s2, op0, op1), rd, wr)

    def tt(out, in0, in1, op, rd, wr, eng="dve"):
        Sd.op(eng, lambda e: e.tensor_tensor(out, in0, in1, op), rd, wr)

    def stt(out, in0, scalar, in1, op0, op1, rd, wr, eng="dve"):
        Sd.op(eng, lambda e: e.scalar_tensor_tensor(out, in0, scalar, in1, op0, op1), rd, wr)

    def cp(out, in_, rd, wr, eng="dve"):
        Sd.op(eng, lambda e: e.tensor_copy(out, in_), rd, wr)

    def memset(ap, val, wr, eng="dve"):
        Sd.op(eng, lambda e: e.memset(ap, val), (), wr)

    def recip(out, in_, rd, wr):
        Sd.op("dve", lambda e: e.reciprocal(out, in_), rd, wr)

    def bcast_ap(handle, off, n):
        return bass.AP(handle, off, [[0, 128], [1, n]])

    NW = 3
    WBYTES = max(KC * 256 * 2, (c.SLAB // 128) * 512 * 2, (KC // 2) * 512 * 2)
    wslots = []
    for i in range(NW):
        off = A.alloc(WBYTES)
        wslots.append((off, T()))
    wring = Ring(wslots)

    ident = alloc(BF16, [128])
    T_const = T()
    invf = alloc(F32, [1])
    sgn = alloc(F32, [1])
    validt = alloc(F32, [S // 128])
    lamt = alloc(F32, [1])
    sub08 = alloc(F32, [256])
    gpc = {n: alloc(F32, [KC]) for n in ("g_mix_pre", "g_mem_pre", "g_mem_kv", "g_mlp_pre")}
    KTm = alloc(BF16, [c.HM, c.MEM])
    VMa = alloc(BF16, [c.MEM // 128, c.HM, 129])
    T_mem = T()
    small = alloc(F32, [64])
    T_small = T()
    G_TOP = A.top

    def phase0():
        Sd.dma("pool", ident, ident_in.ap(), (), [T_const], semT=T_const)
        Sd.dma("sp", invf, invf_in.ap(), (), [T_const], semT=T_const)
        Sd.dma("sp", sgn, sgn_in.ap(), (), [T_const], semT=T_const)
        Sd.dma("sp", validt, valid.ap(), (), [T_const], semT=T_const)
        for n in gpc:
            Sd.dma("sp", gpc[n], gpc_in[n].ap(), (), [T_const], semT=T_const)
        Sd.dma("sp", sub08, bcast_ap(subln, 0, 256), (), [T_const], semT=T_const)
        tmp = alloc(F32, [4, 128])
        Tt = T()
        for i, n in enumerate(("lq1", "lk1", "lq2", "lk2")):
            Sd.dma("sp", tmp[:, i, :], bcast_ap(lam_in[n], 0, 128), (), [Tt], semT=Tt)
        tt(tmp[:, 0, :], tmp[:, 0, :], tmp[:, 1, :], ALU.mult, [Tt], [Tt])
        tt(tmp[:, 2, :], tmp[:, 2, :], tmp[:, 3, :], ALU.mult, [Tt], [Tt])
        Sd.op("dve", lambda e: e.reduce_sum(small[:, 0:1], tmp[:, 0, :], AX.X), [Tt], [T_small])
        Sd.op("dve", lambda e: e.reduce_sum(small[:, 1:2], tmp[:, 2, :], AX.X), [Tt], [T_small])
        act(small[:, 2:4], small[:, 0:2], AF.Exp, [T_small], [T_small])
        tt(lamt, small[:, 2:3], small[:, 3:4], ALU.subtract, [T_small], [T_const])
        ts(lamt, lamt, LAM_INIT, None, ALU.add, None, [T_const], [T_const])
        ts(sub08, sub08, 1.0 - LAM_INIT, None, ALU.mult, None, [T_const], [T_const])

    def rstd_from_ss(dst, ss, n, rd, wr):
        act(dst, ss, AF.Sqrt, rd, wr, bias=EPS, scale=1.0 / n)
        recip(dst, dst, wr, wr)

    tpr = Ring([0, 1])

    def norm_T(xt, Txt, hb, Thb, gname, hT, ThT, tok_off, sc0):
        ss = small[:, sc0:sc0 + 1]
        rs = small[:, sc0 + 1:sc0 + 2]
        memset(ss, 0.0, [T_small])
        chk(19)
        act(hb, xt, AF.Square, [Txt], [Thb, T_small], accum=ss)
        chk(20)
        rstd_from_ss(rs, ss, D, [T_small], [T_small])
        ts(hb, xt, rs, None, ALU.mult, None, [Txt, T_small], [Thb], eng=POOL_ENG)
        chk(21)
        g = gpc[gname]
        for k0 in range(0, KC, 8):
            b = tpr.next()
            nk = min(8, KC - k0)
            for j in range(nk):
                kc = k0 + j
                tr(bank(b, BF16)[:, j * 128:(j + 1) * 128], hb[:, kc * 128:(kc + 1) * 128], [Thb, T_const], [PB[b]],
                   inc=(j == nk - 1))
            chk(22)
            for j in range(nk):
                kc = k0 + j
                src = bank(b, BF16)[:, j * 128:(j + 1) * 128]
                dst = hT[:, kc, tok_off:tok_off + 128]
                if kc % 2 == 0 and EVAC_MODE != 1 or EVAC_MODE == 2:
                    act(dst, src, EVAC_FUNC, [PB[b], T_const], [ThT], scale=g[:, kc:kc + 1])
                else:
                    ts(dst, src, g[:, kc:kc + 1], None, ALU.mult, None, [PB[b], T_const], [ThT])
            chk(23)

    def stream(pieces, PF=2, ring=None):
        slots = [None] * len(pieces)
        if ring is None:
            ring = wring
        else:
            PF = len(ring.items) - 1

        def issue(i):
            sl = ring.next()
            pieces[i][0](sl)
            slots[i] = sl

        for i in range(min(PF, len(pieces))):
            issue(i)
        for i in range(len(pieces)):
            if i + PF < len(pieces):
                issue(i + PF)
            pieces[i][1](slots[i])

    def wload_fm(slot, w_handle, nkc, c0, ncols, sub_off=0):
        off, Tw = slot
        dst = view(off + sub_off, BF16, [nkc, ncols])
        src = w_handle.ap()[:, c0:c0 + ncols].rearrange("(kc p) c -> p kc c", p=128)
        Sd.dma("pool", dst, src, (), [Tw], semT=Tw)
        return dst

    def wload_rows(slot, w_handle, r0, nkc, c0, ncols):
        off, Tw = slot
        dst = view(off, BF16, [nkc, ncols])
        src = w_handle.ap()[r0:r0 + nkc * 128, c0:c0 + ncols].rearrange("(kc p) c -> p kc c", p=128)
        Sd.dma("pool", dst, src, (), [Tw], semT=Tw)
        return dst

    def phase_AB():
        xs = [(alloc(F32, [D]), T()) for _ in range(2)]
        hbs = [(alloc(BF16, [D]), T()) for _ in range(2)]
        hT = alloc(BF16, [KC, 1024])
        ThT = T(multi=True)
        cos2 = alloc(F32, [1024])
        sin2 = alloc(F32, [1024])
        T_rope = T()
        posi = alloc(I32, [1024])
        posf = alloc(F32, [1024])
        T_pos = T()
        rt_off = [A.alloc(4096) for _ in range(2)]
        ropet = [(view(o_, F32, [512]), view(o_ + 2048, F32, [512]), T()) for o_ in rt_off]
        ang, T_ang = view(rt_off[0], F32, [1024]), ropet[0][2]
        wtmp, T_wtmp = view(rt_off[1], F32, [1024]), ropet[1][2]
        obfs = [(alloc(BF16, [512]), T()) for _ in range(3)]
        vbfs = [(alloc(BF16, [256]), T()) for _ in range(3)]
        r_xs, r_hb, r_rt, r_ob, r_vb = Ring(xs), Ring(hbs), Ring(ropet), Ring(obfs), Ring(vbfs)
        pbr = Ring([2, 3, 4, 5, 6, 7])

        def need_tiles(kind, g):
            if kind == "q":
                return set(range(8))
            if kind in ("kb", "vb"):
                return set(range(32))
            o = [1, 2, 8][g]
            return set(range(32 - o, 32)) | set(range(0, 8 + o))

        for tg in range(S // 1024):
            for t8 in range(8):
                xt, Txt = r_xs.next()
                hb, Thb = r_hb.next()
                tok0 = tg * 1024 + t8 * 128
                Sd.dma("sp", xt, xr.ap()[tok0:tok0 + 128, :], (), [Txt], semT=Txt)
                norm_T(xt, Txt, hb, Thb, "g_mix_pre", hT, ThT, t8 * 128, 4 + 2 * (t8 % 2))
            chk(10)
            if tg == 0:
                Sd.dma("sp", HTO.ap(), hT.rearrange("p a b -> p (a b)"), [ThT], (), semT=ThT)
            chk(11)
            Sd.dma("sp", posi, bcast_ap(pos, tg * 1024, 1024), (), [T_pos], semT=T_pos)
            cp(posf, posi, [T_pos], [T_pos])
            ts(ang, posf, invf, None, ALU.mult, None, [T_pos, T_const], [T_ang])
            ts(posf, ang, 1.0 / (2 * PI), None, ALU.mult, None, [T_ang], [T_pos])
            cp(posi, posf, [T_pos], [T_pos])
            cp(posf, posi, [T_pos], [T_pos])
            stt(ang, posf, -2 * PI, ang, ALU.mult, ALU.add, [T_pos, T_ang], [T_ang])
            for (dst, shift, use_sgn) in ((sin2, 0.0, True), (cos2, 0.5 * PI, False)):
                ts(dst, ang, shift, None, ALU.add, None, [T_ang], [T_rope])
                ts(posf, dst, -PI, 2 * PI, ALU.is_lt, ALU.mult, [T_rope], [T_pos])
                ts(wtmp, dst, PI, -2 * PI, ALU.is_gt, ALU.mult, [T_rope], [T_wtmp])
                tt(dst, dst, posf, ALU.add, [T_rope, T_pos], [T_rope])
                tt(dst, dst, wtmp, ALU.add, [T_rope, T_wtmp], [T_rope])
                ts(dst, dst, -PI_SAFE, PI_SAFE, ALU.max, ALU.min, [T_rope], [T_rope])
                act(dst, dst, AF.Sin, [T_rope], [T_rope])
                if use_sgn:
                    ts(dst, dst, sgn, None, ALU.mult, None, [T_rope, T_const], [T_rope])

            chk(12)
            pieces = []

            def add_qk(kind, g, col0, head0):
                tl = need_tiles(kind, g)
                tbs = []
                for tb in (0, 1):
                    need = [t for t in range(4) if (tg * 8 + tb * 4 + t) in tl]
                    if need:
                        assert need == list(range(need[0], need[-1] + 1))
                        tbs.append((tb * 512 + need[0] * 128, len(need) * 128))
                if not tbs:
                    return
                dstT = QT if kind == "q" else KT

                def load(sl, col0=col0):
                    wload_fm(sl, w_in, KC, col0, 256)

                def comp(sl, tbs=tbs, head0=head0, dstT=dstT, kind=kind):
                    off, Tw = sl
                    w = view(off, BF16, [KC, 256])
                    for hh in range(2):
                        for (o0, n_) in tbs:
                            b = pbr.next()
                            ps = bank(b)[:, 0:n_]
                            for kc in range(KC):
                                mm(ps, w[:, kc, hh * 128:(hh + 1) * 128], hT[:, kc, o0:o0 + n_],
                                   kc == 0, kc == KC - 1, [Tw, ThT], [PB[b]], kc == KC - 1)
                            tA, tB, Trt = r_rt.next()
                            ob, Tob = r_ob.next()
                            tA, tB, ob = tA[:, 0:n_], tB[:, 0:n_], ob[:, 0:n_]
                            cs = cos2[:, o0:o0 + n_]
                            sn = sin2[:, o0:o0 + n_]
                            tt(tA, ps, cs, ALU.mult, [PB[b], T_rope], [Trt])
                            tt(tB[0:64, :], ps[64:128, :], sn[64:128, :], ALU.mult, [PB[b], T_rope], [Trt])
                            tt(tB[64:128, :], ps[0:64, :], sn[0:64, :], ALU.mult, [PB[b], T_rope], [Trt])
                            tt(ob, tA, tB, ALU.add, [Trt], [Tob])
                            if kind == "q":
                                dst = dstT.ap()[head0 + hh][:, o0:o0 + n_]
                            else:
                                t0 = tg * 1024 + o0
                                dst = dstT.ap()[head0 + hh][:, t0:t0 + n_]
                            Sd.dma("sp", dst, ob, [Tob], (), semT=Tob)

                pieces.append((load, comp))

            def add_v(kind, g, col0, head0):
                tl = need_tiles(kind, g)
                t8s = [t8 for t8 in range(8) if (tg * 8 + t8) in tl]
                if not t8s:
                    return

                def load(sl, col0=col0):
                    wload_fm(sl, w_in, KC, col0, 256)

                def comp(sl, t8s=t8s, head0=head0, kind=kind):
                    off, Tw = sl
                    w = view(off, BF16, [KC, 256])
                    for t8 in t8s:
                        b = pbr.next()
                        for kc in range(KC):
                            mm(bank(b)[:, 0:256], hT[:, kc, t8 * 128:(t8 + 1) * 128], w[:, kc, :],
                               kc == 0, kc == KC - 1, [Tw, ThT], [PB[b]], kc == KC - 1)
                        vb, Tvb = r_vb.next()
                        tok0 = tg * 1024 + t8 * 128
                        ch = tok0 // 128
                        if kind == "va":
                            ts(vb, bank(b)[:, 0:256], validt[:, ch:ch + 1], None, ALU.mult, None, [PB[b], T_const], [Tvb])
                            Sd.dma("sp", VA.ap()[head0][tok0:tok0 + 128, :], vb[:, 0:128], [Tvb], (), semT=Tvb)
                            Sd.dma("sp", VA.ap()[head0 + 1][tok0:tok0 + 128, :], vb[:, 128:256], [Tvb], (), semT=Tvb)
                        else:
                            act(vb, bank(b)[:, 0:256], AF.Copy, [PB[b]], [Tvb])
                            Sd.dma("sp", VB.ap()[head0][tok0:tok0 + 128, :], vb, [Tvb], (), semT=Tvb)

                pieces.append((load, comp))

            for g in range(3):
                for hp in range(c.HA // 2):
                    h0 = g * c.HA + hp * 2
                    add_qk("q", g, c.o_qa + h0 * 128, h0)
                    add_qk("ka", g, c.o_ka + h0 * 128, h0)
                    add_v("va", g, c.o_va + h0 * 128, h0)
            for h in range(c.HB):
                if stop == 13:
                    break
                add_qk("q", 9, c.o_qb + h * 256, 3 * c.HA + 2 * h)
                add_qk("kb", 9, c.o_kb + h * 256, 3 * c.HA + 2 * h)
                add_v("vb", 9, c.o_vb + h * 256, h)
            stream(pieces)
            chk(13)
            chk(14)

    def phase_C():
        scale = 128.0 ** -0.5
        mark0 = A.top
        p_ring = Ring([(alloc(BF16, [512]), T()) for _ in range(8)])
        st_ring = Ring([4, 5, 6])
        tp_ring = Ring([7])
        SKEW = 2
        accs = [0, 1, 2, 3]
        ostage = Ring([(alloc(BF16, [256]), T()) for _ in range(10)])
        oT_stage = Ring([(alloc(BF16, [2, 128]), T()) for _ in range(4)])
        accsb = Ring([(alloc(F32, [260]), T()) for _ in range(8)])
        pending = []

        def flush():
            while pending:
                pending.pop(0)()

        def evac_accs(E):
            res = []
            for j in range(4):
                sb, Tsb = accsb.next()
                if j % 2 == 0:
                    act(sb[:, 0:E + 1], bank(accs[j])[:, 0:E + 1], AF.Copy, [PB[accs[j]]], [Tsb])
                else:
                    cp(sb[:, 0:E + 1], bank(accs[j])[:, 0:E + 1], [PB[accs[j]]], [Tsb])
                res.append((sb, Tsb))
            return res

        mark = A.top

        masks = alloc(BF16, [NM, 512])
        Sd.dma("pool", masks, masks_in.ap().rearrange("m p f -> p m f"), (), [T_const], semT=T_const)
        offs = [128, 256, 1024]
        spans = [1024 + 2 * o for o in offs]
        sets = []
        for _ in range(2):
            d = dict(
                q=alloc(BF16, [3, OWN]), Tq=T(),
                k=[alloc(BF16, [spans[g]]) for g in range(3)], Tk=T(),
                v=[alloc(BF16, [spans[g] // 128, 129]) for g in range(3)], Tv=T(),
            )
            sets.append(d)

        def load_dil(h, d):
            for g in range(3):
                hq = g * c.HA + h
                o = offs[g]
                Sd.dma("sp", d["q"][:, g, :], QT.ap()[hq], (), [d["Tq"]], semT=d["Tq"])
                Sd.dma("sp", d["k"][g][:, 0:o], KT.ap()[hq][:, S - o:S], (), [d["Tk"]], semT=d["Tk"])
                Sd.dma("sp", d["k"][g][:, o:o + 1024 + o], KT.ap()[hq][:, 0:1024 + o], (), [d["Tk"]], semT=d["Tk"])
                nw = o // 128
                Sd.dma("sp", d["v"][g][:, 0:nw, 0:128],
                       VA.ap()[hq][S - o:S, :].rearrange("(c p) e -> p c e", p=128), (), [d["Tv"]], semT=d["Tv"])
                Sd.dma("sp", d["v"][g][:, nw:nw + 8 + nw, 0:128],
                       VA.ap()[hq][0:1024 + o, :].rearrange("(c p) e -> p c e", p=128), (), [d["Tv"]], semT=d["Tv"])
                cp(d["v"][g][:, 0:nw, 128], validt[:, S // 128 - nw:S // 128], [T_const], [d["Tv"]])
                cp(d["v"][g][:, nw:nw + 8 + nw, 128], validt[:, 0:8 + nw], [T_const], [d["Tv"]])

        def finish_dil(h, qb):
            staged = evac_accs(128)
            obs = []
            for j in range(4):
                sb, Tsb = staged[j]
                rc = small[:, 16 + j:17 + j]
                recip(rc, sb[:, 128:129], [Tsb], [T_small])
                ob, Tob = ostage.next()
                ts(ob[:, 0:128], sb[:, 0:128], rc, None, ALU.mult, None, [Tsb, T_small], [Tob])
                obs.append((ob, Tob))

            def part2():
                for j in range(4):
                    ob, Tob = obs[j]
                    b = tp_ring.next()
                    tr(bank(b, BF16)[:, 0:128], ob[:, 0:128], [Tob, T_const], [PB[b]], True)
                    oT, ToT = oT_stage.next()
                    cp(oT[:, 0, :], bank(b, BF16)[:, 0:128], [PB[b]], [ToT])
                    t0 = qb * 512 + j * 128
                    Sd.dma("sp", OAT.ap()[:, h * OWN + t0:h * OWN + t0 + 128], oT[:, 0, :], [ToT], (), semT=ToT)

            pending.append(part2)

        load_dil(0, sets[0])
        for h in range(c.HA):
            d = sets[h % 2]
            if h + 1 < c.HA:
                load_dil(h + 1, sets[(h + 1) % 2])
            for qb in range(2):
                for g in range(3):
                    gch = []
                    for (delta, mi) in plan[g]:
                        li = qb * 512 + delta + offs[g]
                        gch.append((d["k"][g][:, li:li + 128], d["Tk"], d["v"][g][:, li // 128, :], d["Tv"],
                                    masks[:, mi, :]))
                    attend_grp(d["q"][:, g, qb * 512:(qb + 1) * 512], d["Tq"], gch, 128, accs, st_ring, p_ring,
                               scale, g == 0, g == 2, skew=SKEW, hook=(flush if g == 0 else None))
                finish_dil(h, qb)
        flush()
        Sd.barrier(new_epoch=True)
        A.top = mark
        chk(2)

        NCH = S // 128
        sets = []
        for _ in range(2):
            d = dict(q=alloc(BF16, [2, OWN]), Tq=T(), k=alloc(BF16, [2, S]), Tk=T(),
                     v=alloc(BF16, [NCH, 257]), Tv=T())
            sets.append(d)
        for d in sets:
            memset(d["v"][:, :, 256:257], 1.0, [d["Tv"]])
        on1 = alloc(F32, [4, 256])
        T_on1 = [T() for _ in range(4)]
        ods = Ring([(alloc(F32, [256]), T()) for _ in range(2)])

        def load_diff(h, d):
            for cc in range(2):
                hq = 3 * c.HA + 2 * h + cc
                Sd.dma("sp", d["q"][:, cc, :], QT.ap()[hq], (), [d["Tq"]], semT=d["Tq"])
                Sd.dma("sp", d["k"][:, cc, :], KT.ap()[hq], (), [d["Tk"]], semT=d["Tk"])
            Sd.dma("sp", d["v"][:, :, 0:256], VB.ap()[h].rearrange("(c p) e -> p c e", p=128), (), [d["Tv"]],
                   semT=d["Tv"])

        def finish_diff(h, qb, cc):
            staged = evac_accs(256)
            obs = []
            for j in range(4):
                sb, Tsb = staged[j]
                rc = small[:, 16 + j:17 + j]
                recip(rc, sb[:, 256:257], [Tsb], [T_small])
                if cc == 0:
                    ts(on1[:, j, :], sb[:, 0:256], rc, None, ALU.mult, None, [Tsb, T_small], [T_on1[j]])
                    continue
                od, T_od = ods.next()
                tt(rc, rc, lamt, ALU.mult, [T_small, T_const], [T_small])
                ts(od, sb[:, 0:256], rc, None, ALU.mult, None, [Tsb, T_small], [T_od])
                tt(od, on1[:, j, :], od, ALU.subtract, [T_on1[j], T_od], [T_od])
                ss = small[:, 24 + j:25 + j]
                rs = small[:, 28 + j:29 + j]
                ob, Tob = ostage.next()
                memset(ss, 0.0, [T_small])
                act(ob, od, AF.Square, [T_od], [Tob, T_small], accum=ss)
                rstd_from_ss(rs, ss, 256, [T_small], [T_small])
                stt(ob, od, rs, sub08, ALU.mult, ALU.mult, [T_od, T_small, T_const], [Tob])
                obs.append((ob, Tob))
            if cc == 0:
                return

            def part2():
                for j in range(4):
                    ob, Tob = obs[j]
                    oT, ToT = oT_stage.next()
                    for e2 in range(2):
                        b = tp_ring.next()
                        tr(bank(b, BF16)[:, 0:128], ob[:, e2 * 128:(e2 + 1) * 128], [Tob, T_const], [PB[b]], True)
                        cp(oT[:, e2, :], bank(b, BF16)[:, 0:128], [PB[b]], [ToT])
                    t0 = qb * 512 + j * 128
                    for e2 in range(2):
                        fc = 2 * h + e2
                        Sd.dma("sp", OBT.ap()[:, fc * OWN + t0:fc * OWN + t0 + 128], oT[:, e2, :], [ToT], (),
                               semT=ToT)

            pending.append(part2)

        load_diff(0, sets[0])
        for h in range(c.HB):
            d = sets[h % 2]
            if h + 1 < c.HB:
                load_diff(h + 1, sets[(h + 1) % 2])
            for qb in range(2):
                for cc in range(2):
                    chunks = [(d["k"][:, cc, ci * 128:(ci + 1) * 128], d["Tk"], d["v"][:, ci, :], d["Tv"], None)
                              for ci in range(NCH)]
                    attend_grp(d["q"][:, cc, qb * 512:(qb + 1) * 512], d["Tq"], chunks, 256, accs, st_ring, p_ring,
                               scale, True, True, skew=SKEW, hook=flush)
                    finish_diff(h, qb, cc)
        flush()
        Sd.barrier()
        A.top = mark0

    def attend_grp(qT, Tq, chunks, E, accs, st_ring, p_ring, scale, first, last, skew=1, hook=None):
        n = len(chunks)
        stb = [None] * n

        def qk(ci):
            b = st_ring.next()
            stb[ci] = b
            mm(bank(b), chunks[ci][0], qT, True, True, [chunks[ci][1], Tq], [PB[b]], True)

        for ci in range(min(skew, n)):
            qk(ci)
        for ci in range(n):
            if ci + skew < n:
                qk(ci + skew)
            kt, Tk, va, Tv, mk = chunks[ci]
            b = stb[ci]
            pt, Tp = p_ring.next()
            act(pt, bank(b), AF.Exp, [PB[b]], [Tp], scale=scale)
            if mk is not None:
                tt(pt, pt, mk, ALU.mult, [Tp, T_const], [Tp])
            for j in range(4):
                mm(bank(accs[j])[:, 0:E + 1], pt[:, j * 128:(j + 1) * 128], va, first and ci == 0,
                   last and ci == n - 1, [Tp, Tv], [PB[accs[j]]], j == 3)
            if hook is not None and ci == min(5, n - 1):
                hook()

    def phase_M():
        mark = A.top
        xs = [(alloc(F32, [D]), T()) for _ in range(2)]
        hbs = [(alloc(BF16, [D]), T()) for _ in range(2)]
        mT = alloc(BF16, [KC, c.MEM])
        TmT = T(multi=True)
        for t in range(c.MEM // 128):
            xt, Txt = xs[t % 2]
            hb, Thb = hbs[t % 2]
            Sd.dma("sp", xt, memx.ap()[t * 128:(t + 1) * 128, :], (), [Txt], semT=Txt)
            norm_T(xt, Txt, hb, Thb, "g_mem_kv", mT, TmT, t * 128, 4 + 2 * t)
        memset(VMa[:, :, :, 128:129], 1.0, [T_mem])
        pbr = Ring([2, 3, 4, 5])
        pieces = []
        for hm in range(c.HM):
            def loadk(sl, hm=hm):
                wload_fm(sl, w_mkv, KC, hm * 128, 128)

            def compk(sl, hm=hm):
                off, Tw = sl
                w = view(off, BF16, [KC, 128])
                b = pbr.next()
                for kc in range(KC):
                    mm(bank(b)[:, 0:c.MEM], w[:, kc, :], mT[:, kc, :], kc == 0, kc == KC - 1, [Tw, TmT], [PB[b]],
                       kc == KC - 1)
                act(KTm[:, hm, :], bank(b)[:, 0:c.MEM], AF.Copy, [PB[b]], [T_mem])

            def loadv(sl, hm=hm):
                wload_fm(sl, w_mkv, KC, c.MW + hm * 128, 128)

            def compv(sl, hm=hm):
                off, Tw = sl
                w = view(off, BF16, [KC, 128])
                for t in range(c.MEM // 128):
                    b = pbr.next()
                    for kc in range(KC):
                        mm(bank(b)[:, 0:128], mT[:, kc, t * 128:(t + 1) * 128], w[:, kc, :], kc == 0, kc == KC - 1,
                           [Tw, TmT], [PB[b]], kc == KC - 1)
                    act(VMa[:, t, hm, 0:128], bank(b)[:, 0:128], AF.Copy, [PB[b]], [T_mem])

            pieces.append((loadk, compk))
            pieces.append((loadv, compv))
        stream(pieces)
        Sd.barrier()
        A.top = mark

    def norm_residual(ybuf, Ty, src_dram, dst_dram, gpost_name, gpre_name, hTn, ThTn, tokbase):
        gpost = alloc(F32, [D])
        xts = [(alloc(F32, [D]), T()) for _ in range(1)]
        hbs2 = [(alloc(BF16, [D]), T()) for _ in range(1)]
        Tg = T()
        Sd.dma("sp", gpost, bcast_ap(gv[gpost_name], 0, D), (), [Tg], semT=Tg)
        for t4 in range(4):
            xt, Txt = xts[t4 % len(xts)]
            hb, Thb = hbs2[t4 % len(hbs2)]
            tok0 = tokbase + t4 * 128
            Sd.dma("sp", xt, src_dram.ap()[tok0:tok0 + 128, :], (), [Txt], semT=Txt)
            ss = small[:, 32 + 2 * (t4 % 2):33 + 2 * (t4 % 2)]
            rs = small[:, 33 + 2 * (t4 % 2):34 + 2 * (t4 % 2)]
            y = ybuf[:, t4, :]
            memset(ss, 0.0, [T_small])
            act(hb, y, AF.Square, [Ty], [Thb, T_small], accum=ss)
            rstd_from_ss(rs, ss, D, [T_small], [T_small])
            stt(y, y, rs, gpost, ALU.mult, ALU.mult, [Ty, T_small, Tg], [Ty])
            tt(xt, xt, y, ALU.add, [Txt, Ty], [Txt], eng=POOL_ENG)
            Sd.dma("sp", dst_dram.ap()[tok0:tok0 + 128, :], xt, [Txt], (), semT=Txt)
            if gpre_name is not None:
                norm_T(xt, Txt, hb, Thb, gpre_name, hTn, ThTn, t4 * 128, 36 + 2 * (t4 % 2))

    MLP_T = [T() for _ in range(MLP_EXTRA)]

    def post_half(hf):
        tokbase = hf * 512
        mark = A.top
        scale = 128.0 ** -0.5
        NA, NB = c.AW // 128, c.BW // 128
        hTreg = alloc(BF16, [KC, 512])
        yoff = A.alloc(4 * D * 4)
        ybuf = view(yoff, F32, [4, D])
        Ty = T(multi=True)
        Xbase = A.top

        hTo = hTreg
        T_in = T()
        mT = alloc(BF16, [KC, 512])
        TmT = T(multi=True)
        oaT = alloc(BF16, [NA, 512])
        o2 = NB * 512 * 2
        soff = yoff if o2 + 4 * 2048 <= 4 * D * 4 else A.alloc(o2 + 4 * 2048)
        obT = view(soff, BF16, [NB, 512])
        sg = [[view(soff + o2 + (k * 2 + j) * 2048, F32, [512]) for j in range(2)] for k in range(2)]
        Tsg = [[T() for j in range(2)] for k in range(2)]
        Sd.dma("sp", hTo, HTO.ap().rearrange("p (a b) -> p a b", a=KC)[:, :, tokbase:tokbase + 512], (), [T_in],
               semT=T_in)
        Sd.dma("sp", oaT, OAT.ap().rearrange("p (a b) -> p a b", a=NA)[:, :, tokbase:tokbase + 512], (), [T_in],
               semT=T_in)
        Sd.dma("sp", obT, OBT.ap().rearrange("p (a b) -> p a b", a=NB)[:, :, tokbase:tokbase + 512], (), [T_in],
               semT=T_in)
        pieces = []
        for cc in range(D // 256):
            def load_g(sl, cc=cc, k=0):
                wload_fm(sl, w_in, KC, (c.o_ga if k == 0 else c.o_gb) + cc * 256, 256)

            def comp_g(sl, cc=cc, k=0):
                off, Tw = sl
                w = view(off, BF16, [KC, 256])
                for j in range(2):
                    b = 2 * k + j
                    for kc in range(KC):
                        mm(bank(b), w[:, kc, j * 128:(j + 1) * 128], hTo[:, kc, :], kc == 0, kc == KC - 1,
                           [Tw, T_in], [PB[b]], kc == KC - 1)
                    act(sg[k][j], bank(b), AF.Sigmoid, [PB[b]], [Tsg[k][j]])

            def load_ab(sl, cc=cc):
                wload_fm(sl, w_a, NA, cc * 256, 256)
                wload_fm(sl, w_b, NB, cc * 256, 256, sub_off=NA * 256 * 2)

            def comp_ab(sl, cc=cc):
                off_a, Tab = sl
                wa = view(off_a, BF16, [NA, 256])
                wb = view(off_a + NA * 256 * 2, BF16, [NB, 256])
                for j in range(2):
                    cs = slice(j * 128, (j + 1) * 128)
                    b2, b3 = 4 + j, 6 + j
                    for kc in range(NA):
                        mm(bank(b2), wa[:, kc, cs], oaT[:, kc, :], kc == 0, kc == NA - 1, [Tab, T_in], [PB[b2]],
                           kc == NA - 1)
                    for kc in range(NB):
                        mm(bank(b3), wb[:, kc, cs], obT[:, kc, :], kc == 0, kc == NB - 1, [Tab, T_in], [PB[b3]],
                           kc == NB - 1)
                    tt(sg[0][j], sg[0][j], bank(b2), ALU.mult, [Tsg[0][j], PB[b2]], [Tsg[0][j]])
                    tt(sg[1][j], sg[1][j], bank(b3), ALU.mult, [Tsg[1][j], PB[b3]], [Tsg[1][j]])
                    tt(mT[:, cc * 2 + j, :], sg[0][j], sg[1][j], ALU.add, [Tsg[0][j], Tsg[1][j]], [TmT])

            pieces.append((load_g, comp_g))
            pieces.append((lambda sl, cc=cc: load_g(sl, cc, 1), lambda sl, cc=cc: comp_g(sl, cc, 1)))
            pieces.append((load_ab, comp_ab))
        stream(pieces)
        Sd.barrier()

        def proj_tm(aT, TaT, nkc_total, w_handle, row0):
            nsp = 2 if nkc_total >= 2 and nkc_total * 512 * 2 > WBYTES else 1
            nk = nkc_total // nsp
            pcs = []
            pbs = Ring([(0, 1, 2, 3), (4, 5, 6, 7)])
            state = {}
            for ct in range(D // 512):
                for kh in range(nsp):
                    def load(sl, ct=ct, kh=kh):
                        wload_rows(sl, w_handle, row0 + kh * nk * 128, nk, ct * 512, 512)

                    def comp(sl, ct=ct, kh=kh):
                        off, Tw = sl
                        w = view(off, BF16, [nk, 512])
                        if kh == 0:
                            state["b"] = pbs.next()
                        bs = state["b"]
                        for t4 in range(4):
                            for kl in range(nk):
                                kc = kh * nk + kl
                                mm(bank(bs[t4]), aT[:, kc, t4 * 128:(t4 + 1) * 128], w[:, kl, :],
                                   kc == 0, kc == nkc_total - 1, [TaT, Tw], [PB[bs[t4]]], kl == nk - 1)
                        if kh == nsp - 1:
                            for t4 in range(4):
                                dst = ybuf[:, t4, ct * 512:(ct + 1) * 512]
                                if t4 % 2 == 0:
                                    act(dst, bank(bs[t4]), AF.Copy, [PB[bs[t4]]], [Ty])
                                else:
                                    cp(dst, bank(bs[t4]), [PB[bs[t4]]], [Ty])

                    pcs.append((load, comp))
            stream(pcs)

        proj_tm(mT, TmT, KC, w_mix, 0)
        Sd.barrier()
        A.top = Xbase
        h2T = hTreg
        Th2 = T(multi=True)
        norm_residual(ybuf, Ty, xr, X1, "g_mix_post", "g_mem_pre", h2T, Th2, tokbase)
        Sd.barrier()
        A.top = Xbase
        QTm = alloc(BF16, [c.HM, 512])
        TQm = T()
        oTm = alloc(BF16, [c.HM, 512])
        ToTm = T(multi=True)
        p_ring = Ring([(alloc(BF16, [512]), T()) for _ in range(4)])
        ost = Ring([(alloc(BF16, [128]), T()) for _ in range(2)])
        pbr = Ring([6, 7])
        pieces = []
        for hm in range(c.HM):
            def load(sl, hm=hm):
                wload_fm(sl, w_mq, KC, hm * 128, 128)

            def comp(sl, hm=hm):
                off, Tw = sl
                w = view(off, BF16, [KC, 128])
                b = pbr.next()
                for kc in range(KC):
                    mm(bank(b), w[:, kc, :], h2T[:, kc, :], kc == 0, kc == KC - 1, [Tw, Th2], [PB[b]], kc == KC - 1)
                act(QTm[:, hm, :], bank(b), AF.Copy, [PB[b]], [TQm])

            pieces.append((load, comp))
        stream(pieces)
        st_ring = Ring([4, 5])
        accs = [0, 1, 2, 3]
        tp_ring = Ring([6, 7])
        for hm in range(c.HM):
            chunks = [(KTm[:, hm, ci * 128:(ci + 1) * 128], T_mem, VMa[:, ci, hm, :], T_mem, None)
                      for ci in range(c.MEM // 128)]
            attend_grp(QTm[:, hm, :], TQm, chunks, 128, accs, st_ring, p_ring, scale, True, True)
            for j in range(4):
                a = bank(accs[j])
                rc = small[:, 16 + j:17 + j]
                recip(rc, a[:, 128:129], [PB[accs[j]]], [T_small])
                ob, Tob = ost.next()
                ts(ob, a[:, 0:128], rc, None, ALU.mult, None, [PB[accs[j]], T_small], [Tob])
                b = tp_ring.next()
                tr(bank(b, BF16)[:, 0:128], ob, [Tob, T_const], [PB[b]], True)
                cp(oTm[:, hm, j * 128:(j + 1) * 128], bank(b, BF16)[:, 0:128], [PB[b]], [ToTm])
        Sd.barrier()
        proj_tm(oTm, ToTm, c.HM, w_mo, 0)
        Sd.barrier()
        A.top = Xbase
        h3T = hTreg
        Th3 = T(multi=True)
        norm_residual(ybuf, Ty, X1, X2, "g_mem_post", "g_mlp_pre", h3T, Th3, tokbase)
        Sd.barrier()
        A.top = Xbase
        NFC = c.SLAB // 128
        uT = alloc(BF16, [NFC, 512])
        TuT = T(multi=True)
        rtmp = Ring([(alloc(F32, [512]), T()) for _ in range(2)])
        pbu = Ring([4, 5, 6, 7])
        mlp_ring = Ring(list(wslots) + [(A.alloc(WBYTES), MLP_T[i]) for i in range(MLP_EXTRA)])
        for sl_i in range(c.FF // c.SLAB):
            ff0 = sl_i * c.SLAB
            pieces = []
            HK = KC // 2
            for grp in range(c.SLAB // 512):
                for kh in range(2):
                    def load(sl, grp=grp, kh=kh, ff0=ff0):
                        wload_rows(sl, w_up, kh * HK * 128, HK, ff0 + grp * 512, 512)

                    def comp(sl, grp=grp, kh=kh):
                        off, Tw = sl
                        w = view(off, BF16, [HK, 512])
                        for j in range(4):
                            b = 4 + j
                            for kl in range(HK):
                                kc = kh * HK + kl
                                mm(bank(b), w[:, kl, j * 128:(j + 1) * 128], h3T[:, kc, :], kc == 0, kc == KC - 1,
                                   [Tw, Th3], [PB[b]], kl == HK - 1)
                            if kh == 1:
                                r, Tr = rtmp.next()
                                act(r, bank(b), AF.Relu, [PB[b]], [Tr])
                                tt(uT[:, grp * 4 + j, :], r, r, ALU.mult, [Tr], [TuT])

                    pieces.append((load, comp))
            for ct in range(D // 512):
                def load(sl, ct=ct, ff0=ff0):
                    wload_rows(sl, w_dn, ff0, NFC, ct * 512, 512)

                def comp(sl, ct=ct, first=(sl_i == 0)):
                    off, Tw = sl
                    w = view(off, BF16, [NFC, 512])
                    for t4 in range(4):
                        b = t4
                        for fc in range(NFC):
                            mm(bank(b), uT[:, fc, t4 * 128:(t4 + 1) * 128], w[:, fc, :], fc == 0, fc == NFC - 1,
                               [TuT, Tw], [PB[b]], fc == NFC - 1)
                        dst = ybuf[:, t4, ct * 512:(ct + 1) * 512]
                        if first:
                            cp(dst, bank(b), [PB[b]], [Ty])
                        else:
                            tt(dst, dst, bank(b), ALU.add, [Ty, PB[b]], [Ty])

                pieces.append((load, comp))
            stream(pieces, ring=mlp_ring)
        Sd.barrier()
        A.top = Xbase
        norm_residual(ybuf, Ty, X2, out, "g_mlp_post", None, None, None, tokbase)
        Sd.barrier(new_epoch=True)
        A.top = mark

    try:
        phase0()
        chk(0)
        markAB = A.top
        phase_AB()
        Sd.barrier(new_epoch=True)
        chk(1)
        A.top = markAB
        phase_C()
        chk(3)
        phase_M()
        chk(4)
        for hf in range(2):
            post_half(hf)
    except _Stop:
        pass
    Sd.barrier()

    with nc.Block() as block:
        Sd.emit(block)
    es.close()
    return nc


def make_in_maps(cfg, x, mem, positions, norm_mix_pre, w_in, w_a, w_b, w_mix_out, norm_mix_post,
                 lambda_q1, lambda_k1, lambda_q2, lambda_k2, diff_subln,
                 norm_mem_pre, norm_mem_kv, w_mem_q, w_mem_kv, w_mem_o, norm_mem_post,
                 norm_mlp_pre, w_mlp_up, w_mlp_down, norm_mlp_post):
    c = cfg
    S = c.S
    f = lambda a: np.ascontiguousarray(np.asarray(a, dtype=np.float32))
    inv = (10000.0 ** (-np.arange(0, 128, 2, dtype=np.float32) / np.float32(128))).astype(np.float32)
    invf = np.concatenate([inv, inv]).reshape(128, 1).astype(np.float32)
    sgn = np.concatenate([np.ones(64), -np.ones(64)]).reshape(128, 1).astype(np.float32)
    shared = dict(
        w_in=f(w_in[0]), w_a=f(w_a[0]), w_b=f(w_b[0]), w_mix=f(w_mix_out[0]), w_mq=f(w_mem_q[0]),
        w_mkv=f(w_mem_kv[0]), w_mo=f(w_mem_o[0]), w_up=f(w_mlp_up[0]), w_dn=f(w_mlp_down[0]),
        g_mix_pre=f(norm_mix_pre), g_mix_post=f(norm_mix_post), g_mem_pre=f(norm_mem_pre),
        g_mem_kv=f(norm_mem_kv), g_mem_post=f(norm_mem_post), g_mlp_pre=f(norm_mlp_pre),
        g_mlp_post=f(norm_mlp_post), subln=f(diff_subln), lq1=f(lambda_q1), lk1=f(lambda_k1),
        lq2=f(lambda_q2), lk2=f(lambda_k2), ident=np.eye(128, dtype=np.float32), invf=invf, sgn=sgn,
        masks=build_masks(),
    )
    for n in ("g_mix_pre", "g_mem_pre", "g_mem_kv", "g_mlp_pre"):
        shared["pc_" + n] = np.ascontiguousarray(shared[n].reshape(c.KC, 128).T)
    x = np.asarray(x, dtype=np.float32)
    mem = np.asarray(mem, dtype=np.float32)
    positions = np.asarray(positions, dtype=np.int32)
    maps = []
    for core in range(NCORES):
        b, r = core // 4, core % 4
        sh = OWN * r
        xr = np.ascontiguousarray(np.roll(x[b], -sh, axis=0))
        pr = np.ascontiguousarray(np.roll(positions[b], -sh)).reshape(1, S)
        j = np.arange(S)
        real = np.where(j < 2048, sh + j, sh + j - S)
        v = ((real >= 0) & (real < S) & ((j < 2048) | (j >= 3072))).astype(np.float32)
        valid = np.ascontiguousarray(v.reshape(S // 128, 128).T)
        m = dict(shared)
        m.update(xr=xr, pos=pr, valid=valid, mem=np.ascontiguousarray(mem[b]))
        maps.append(m)
    return maps


_NC_CACHE = {}


def run(cfg, inputs, stop=None):
    key = (cfg.D, cfg.HA, cfg.HB, cfg.HM, cfg.FF, stop)
    if key not in _NC_CACHE:
        _NC_CACHE[key] = build(cfg, stop)
    nc = _NC_CACHE[key]
    maps = make_in_maps(cfg, **inputs)
    res = run_bass_kernel_spmd(nc, maps, core_ids=list(range(NCORES)))
    outp = np.zeros((2, cfg.S, cfg.D), np.float32)
    for core in range(NCORES):
        b, r = core // 4, core % 4
        outp[b, OWN * r:OWN * (r + 1), :] = res.results[core]["out"]
    return outp


def kernel(**inputs):
    return run(Cfg(), inputs)
```
